# Optimizing a Trainium2 kernel written in Bass

```python
import jax, jax.numpy as jnp
from jax import lax
import numpy as np

D_MODEL = 2048
BATCH = 8
SEQ = 2048
DEPTH = 2

N_MIXERS = 2
EXPAND = 2
BRANCH = EXPAND * D_MODEL
HEAD = 64
R_HEADS = BRANCH // HEAD
LORA_DECAY = 96
LORA_ICLR = 96
SHIFT_COLS = 3 * BRANCH + 2 * LORA_DECAY + 2 * LORA_ICLR
R_IN_COLS = SHIFT_COLS + BRANCH
CHUNK = 128
G_GROUPS = 16
G_GROUP_W = BRANCH // G_GROUPS
N_RWKV = (DEPTH + 1) // 2
N_GMLP = DEPTH // 2
RMS_EPS = 1e-5
LN_EPS = 1e-5
GN_EPS = 64e-5

kernel_name = "bidir_rwkv7_sgu_interleaved"


def rms_norm(x, g):
    xf = x.astype(jnp.float32)
    y = xf * lax.rsqrt(jnp.mean(xf * xf, axis=-1, keepdims=True) + RMS_EPS)
    return (y * g.astype(jnp.float32)).astype(x.dtype)


def layer_norm(x, g, b):
    xf = x.astype(jnp.float32)
    mu = jnp.mean(xf, axis=-1, keepdims=True)
    var = jnp.mean(jnp.square(xf - mu), axis=-1, keepdims=True)
    y = (xf - mu) * lax.rsqrt(var + LN_EPS)
    return (y * g.astype(jnp.float32) + b.astype(jnp.float32)).astype(x.dtype)


def shift_prev(z):
    return jnp.pad(z[:, :-1], ((0, 0), (1, 0), (0, 0)))


def shift_next(z):
    return jnp.pad(z[:, 1:], ((0, 0), (0, 1), (0, 0)))


def wkv_scan(r, w, k, v, a, b, reverse):
    Bsz, T, H, N = r.shape
    xs = tuple(jnp.moveaxis(t.astype(jnp.float32), 1, 0) for t in (r, w, k, v, a, b))

    def step(S, inp):
        r_t, w_t, k_t, v_t, a_t, b_t = inp
        sa = jnp.einsum('bhvk,bhk->bhv', S, a_t)
        S = (S * w_t[:, :, None, :]
             + sa[..., None] * b_t[:, :, None, :]
             + v_t[..., None] * k_t[:, :, None, :])
        y = jnp.einsum('bhvk,bhk->bhv', S, r_t)
        return S, y

    S0 = jnp.zeros((Bsz, H, N, N), jnp.float32)
    _, ys = lax.scan(step, S0, xs, reverse=reverse)
    return jnp.moveaxis(ys, 0, 1)


def rwkv7_bidir_mixer(h, w_in, mu_prev, mu_next, w0, w_up, a0, a_up, k_k, k_a, r_k,
                      ln_w, ln_b, w_out):
    Bsz, T, _ = h.shape
    z = h @ w_in
    zs, gate = z[..., :SHIFT_COLS], z[..., SHIFT_COLS:]
    zs = zs + mu_prev * (shift_prev(zs) - zs) + mu_next * (shift_next(zs) - zs)
    r, k, v, dw, da = jnp.split(
        zs, [BRANCH, 2 * BRANCH, 3 * BRANCH, 3 * BRANCH + 2 * LORA_DECAY], axis=-1)
    dw = dw.reshape(Bsz, T, 2, LORA_DECAY)
    da = da.reshape(Bsz, T, 2, LORA_ICLR)
    heads = lambda t: t.reshape(Bsz, T, R_HEADS, HEAD)

    kk = heads(k * k_k).astype(jnp.float32)
    kk = kk / jnp.maximum(jnp.linalg.norm(kk, axis=-1, keepdims=True), 1e-12)
    rh, vh = heads(r), heads(v)

    y = 0.0
    bonus = 0.0
    for d, rev in ((0, False), (1, True)):
        wll = -jax.nn.softplus(-(w0[d] + jnp.tanh(dw[:, :, d]) @ w_up[d])) - 0.5
        decay = jnp.exp(-jnp.exp(wll.astype(jnp.float32)))
        a_d = jax.nn.sigmoid(a0[d] + da[:, :, d] @ a_up[d])
        k_d = k * (1.0 + (a_d - 1.0) * k_a)
        a_h = heads(a_d).astype(jnp.float32)
        kd_h = heads(k_d)
        y = y + wkv_scan(rh, heads(decay), kd_h, vh, -kk, kk * a_h, rev)
        bonus = bonus + jnp.sum(rh * kd_h * r_k, axis=-1, keepdims=True) * vh

    mu = jnp.mean(y, axis=-1, keepdims=True)
    var = jnp.mean(jnp.square(y - mu), axis=-1, keepdims=True)
    yn = ((y - mu) * lax.rsqrt(var + GN_EPS)).reshape(Bsz, T, BRANCH)
    yn = (yn * ln_w.astype(jnp.float32) + ln_b.astype(jnp.float32)).astype(h.dtype)
    out = yn + bonus.reshape(Bsz, T, BRANCH)
    return (out * jax.nn.silu(gate)) @ w_out


def chunked_sgu_mixer(h, w_in, ln_g, ln_b, w_s, b_s, w_out):
    Bsz, T, _ = h.shape
    u, v, gate = jnp.split(h @ w_in, 3, axis=-1)
    u = jax.nn.gelu(u)
    v = layer_norm(jax.nn.gelu(v), ln_g, ln_b)
    vc = v.reshape(Bsz, T // CHUNK, CHUNK, G_GROUPS, G_GROUP_W)
    s = jnp.einsum('gij,bcjgd->bcigd', w_s, vc) + b_s.T[:, :, None]
    s = s.reshape(Bsz, T, BRANCH)
    return (u * s * jax.nn.silu(gate)) @ w_out


def setup_inputs(seed: int = 0) -> dict:
    key = jax.random.key(seed)
    ks = jax.random.split(key, 24)
    f = jnp.float32
    D, E = D_MODEL, BRANCH
    nrm = lambda k, shape, s: jax.random.normal(k, shape, f) * s
    return {
        "x": nrm(ks[0], (BATCH, SEQ, D), 1.0),
        "norm_g": 1.0 + nrm(ks[1], (DEPTH, D), 0.02),
        "final_norm_g": 1.0 + nrm(ks[2], (D,), 0.02),
        "rwkv_w_in": nrm(ks[3], (N_RWKV, D, R_IN_COLS), D ** -0.5),
        "rwkv_mu_prev": jax.random.uniform(ks[4], (N_RWKV, SHIFT_COLS), f, 0.0, 0.5),
        "rwkv_mu_next": jax.random.uniform(ks[5], (N_RWKV, SHIFT_COLS), f, 0.0, 0.5),
        "rwkv_w0": nrm(ks[6], (N_RWKV, 2, E), 0.5),
        "rwkv_w_up": nrm(ks[7], (N_RWKV, 2, LORA_DECAY, E), LORA_DECAY ** -0.5),
        "rwkv_a0": nrm(ks[8], (N_RWKV, 2, E), 0.1),
        "rwkv_a_up": nrm(ks[9], (N_RWKV, 2, LORA_ICLR, E), LORA_ICLR ** -0.5),
        "rwkv_k_k": 0.85 + nrm(ks[10], (N_RWKV, E), 0.02),
        "rwkv_k_a": 1.0 + nrm(ks[11], (N_RWKV, E), 0.02),
        "rwkv_r_k": nrm(ks[12], (N_RWKV, R_HEADS, HEAD), 0.1),
        "rwkv_ln_w": 1.0 + nrm(ks[13], (N_RWKV, E), 0.02),
        "rwkv_ln_b": nrm(ks[14], (N_RWKV, E), 0.02),
        "rwkv_w_out": nrm(ks[15], (N_RWKV, E, D), E ** -0.5),
        "gmlp_w_in": nrm(ks[16], (N_GMLP, D, 3 * E), D ** -0.5),
        "gmlp_ln_g": 1.0 + nrm(ks[17], (N_GMLP, E), 0.02),
        "gmlp_ln_b": nrm(ks[18], (N_GMLP, E), 0.02),
        "gmlp_w_s": nrm(ks[19], (N_GMLP, G_GROUPS, CHUNK, CHUNK), CHUNK ** -0.5),
        "gmlp_b_s": 1.0 + nrm(ks[20], (N_GMLP, G_GROUPS, CHUNK), 0.02),
        "gmlp_w_out": nrm(ks[21], (N_GMLP, E, D), E ** -0.5),
    }


def reference(x, norm_g, final_norm_g,
              rwkv_w_in, rwkv_mu_prev, rwkv_mu_next, rwkv_w0, rwkv_w_up, rwkv_a0, rwkv_a_up,
              rwkv_k_k, rwkv_k_a, rwkv_r_k, rwkv_ln_w, rwkv_ln_b, rwkv_w_out,
              gmlp_w_in, gmlp_ln_g, gmlp_ln_b, gmlp_w_s, gmlp_b_s, gmlp_w_out):
    h = x
    for i in range(DEPTH):
        z = rms_norm(h, norm_g[i])
        j = i // N_MIXERS
        if i % N_MIXERS == 0:
            y = rwkv7_bidir_mixer(z, rwkv_w_in[j], rwkv_mu_prev[j], rwkv_mu_next[j],
                                  rwkv_w0[j], rwkv_w_up[j], rwkv_a0[j], rwkv_a_up[j],
                                  rwkv_k_k[j], rwkv_k_a[j], rwkv_r_k[j],
                                  rwkv_ln_w[j], rwkv_ln_b[j], rwkv_w_out[j])
        else:
            y = chunked_sgu_mixer(z, gmlp_w_in[j], gmlp_ln_g[j], gmlp_ln_b[j],
                                  gmlp_w_s[j], gmlp_b_s[j], gmlp_w_out[j])
        h = h + y
    return rms_norm(h, final_norm_g)
```

```python
import numpy as np
import concourse.bass as bass
import concourse.mybir as mybir
from concourse.bass_utils import run_bass_kernel_spmd

F32 = mybir.dt.float32
BF16 = mybir.dt.bfloat16
AF = mybir.ActivationFunctionType
ALU = mybir.AluOpType
AX = mybir.AxisListType

T = 2048
D = 2048
E = 4096
RIN = 16768
SHIFT = 12672
NPAIR = 32
CH = 64
NCHK = T // CH
CDEC = 0.6065306597126334
RMS_EPS = 1e-5
LN_EPS = 1e-5
GN_EPS = 64e-5
ENGS = ["pe", "act", "dve", "pool", "sp"]
MERGE = True


class Prog:
    def __init__(self):
        self.ops = {e: [] for e in ENGS}
        self.res = {}
        self.order = []
        self.dma_count = {}
        self.gdep = None

    def barrier(self):
        self.gdep = None
        self.gdep = self.emit("sp", lambda e: e.nop(), writes=list(self.res.keys()))

    def emit(self, eng, fn, reads=(), writes=(), dma=None):
        idx = len(self.ops[eng])
        deps = set()
        if self.gdep is not None:
            deps.add(self.gdep)
        for r in reads:
            st = self.res.get(r)
            if st is not None and st[0] is not None:
                deps.add(st[0])
        for w in writes:
            st = self.res.get(w)
            if st is not None:
                if st[0] is not None:
                    deps.add(st[0])
                deps.update(st[1])
        if dma is not None:
            c = self.dma_count.get(dma, 0) + 1
            self.dma_count[dma] = c
            tok = ("d", dma, c)
        else:
            tok = ("e", eng, idx)
        deps.discard(tok)
        self.ops[eng].append(dict(fn=fn, deps=deps, dma=dma, tok=tok, inc=False, waits=[]))
        self.order.append((eng, idx))
        for r in reads:
            st = self.res.get(r)
            if st is None:
                self.res[r] = (None, [tok])
            else:
                st[1].append(tok)
        for w in writes:
            self.res[w] = (tok, [])
        return tok

    def finalize(self):
        clock = {e: {} for e in ENGS}
        snap = {}
        for eng, idx in self.order:
            op = self.ops[eng][idx]
            ck = clock[eng]
            waits = []
            for d in sorted(op["deps"], key=lambda t: (t[0], str(t[1]), t[2])):
                key = (d[0], d[1])
                if d[0] == "e" and d[1] == eng and eng == "pe":
                    continue
                if ck.get(key, -1) >= d[2]:
                    continue
                waits.append(d)
                ck[key] = d[2]
                s = snap.get(d)
                if s:
                    for k, v in s.items():
                        if ck.get(k, -1) < v:
                            ck[k] = v
            op["waits"] = waits
            for d in waits:
                if d[0] == "e":
                    self.ops[d[1]][d[2]]["inc"] = True
            snap[op["tok"]] = dict(ck)
        val = {}
        for e in ENGS:
            c = 0
            for i, op in enumerate(self.ops[e]):
                if op["inc"] and op["dma"] is None:
                    c += 1
                    val[("e", e, i)] = c
        self.val = val

    def replay(self, eng, engine_obj, sems, dma_sems):
        for op in self.ops[eng]:
            for d in op["waits"]:
                if d[0] == "e":
                    engine_obj.wait_ge(sems[d[1]], self.val[d])
                else:
                    engine_obj.wait_ge(dma_sems[d[1]], 16 * d[2])
            ins = op["fn"](engine_obj)
            if ins is None:
                continue
            if op["dma"] is not None:
                ins.then_inc(dma_sems[op["dma"]], 16)
            elif op["inc"]:
                ins.then_inc(sems[eng], 1)


class Arena:
    def __init__(self, nc, base=16512, limit=229344):
        self.nc = nc
        self.off = base
        self.limit = limit
        self.n = 0
        self.peak = 0

    def alloc(self, name, shape, dt):
        esz = 4 if dt == F32 else 2
        size = esz
        for s in shape[1:]:
            size *= s
        off = (self.off + 31) // 32 * 32
        assert off + size <= self.limit, (name, off, size)
        self.n += 1
        h = self.nc.alloc_sbuf_tensor_at("%s_%d" % (name, self.n), list(shape), dt, offset=off)
        self.off = off + size
        self.peak = max(self.peak, self.off)
        return h

    def mark(self):
        return self.off

    def reset(self, m, prog=None):
        self.off = m
        if prog is not None:
            prog.barrier()


CB_IDENT, CB_FOLD, CB_BONES, CB_M4, CB_ML, CB_SGN, CB_M01, CB_RESET, CB_N = 0, 128, 192, 320, 1344, 1600, 1856, 1860, 1860 + 2048
CF_NEG, CF_N = 0, 4
(PB_MUP_R, PB_MUP_K, PB_MUP_V, PB_MUN_R, PB_MUN_K, PB_MUN_V, PB_W0_0, PB_W0_1, PB_A0_0, PB_A0_1,
 PB_KK, PB_KA, PB_RK, PB_LNW, PB_LNB, PB_GLG, PB_GLB, PB_C0_R, PB_C0_K, PB_C0_V, PB_OMKA) = range(21)
PL_MUP, PL_MUN, PL_C0 = 21 * 32, 21 * 32 + 4, 21 * 32 + 8
NPP = 21 * 32 + 12


def _consts():
    p = np.arange(128)
    h = p // 64
    i = p % 64
    same = (h[:, None] == h[None, :]).astype(np.float32)
    cb = np.zeros((128, CB_N), np.float32)
    cb[:, CB_IDENT:CB_IDENT + 128] = np.eye(128)
    cb[:, CB_FOLD:CB_FOLD + 64] = (i[:, None] == np.arange(64)[None, :])
    cb[:, CB_BONES:CB_BONES + 128] = same
    for d in range(2):
        if d == 0:
            strict = (i[:, None] < i[None, :])
            incl = (i[:, None] <= i[None, :])
        else:
            strict = (i[:, None] > i[None, :])
            incl = (i[:, None] >= i[None, :])
        o = CB_M4 + d * 512
        cb[:, o:o + 128] = -same * strict
        cb[:, o + 128:o + 256] = -same * incl
        cb[:, o + 256:o + 384] = same * strict
        cb[:, o + 384:o + 512] = same * incl
        cb[:, CB_ML + d * 128:CB_ML + (d + 1) * 128] = -same * strict.T
    cb[:, CB_SGN:CB_SGN + 128] = -1.0
    cb[:, CB_SGN + 128:CB_SGN + 256] = 1.0
    cb[:, CB_M01] = (h == 0)
    cb[:, CB_M01 + 1] = (h == 1)
    cf = np.zeros((128, CF_N), np.float32)
    cb[:, CB_RESET:CB_RESET + 2048] = 1.0
    cb[:, CB_RESET:CB_RESET + 2048:64] = 0.0
    cf[:, CF_NEG] = np.where(h == 0, 0.0, -30000.0)
    cf[:, CF_NEG + 1] = np.where(h == 1, 0.0, -30000.0)
    return cb, cf


def _cols(v):
    v = np.asarray(v, np.float32).reshape(-1, 128)
    return np.ascontiguousarray(v.T)


def _host_params(inp):
    pp = np.zeros((128, NPP), np.float32)
    mp = inp["rwkv_mu_prev"][0]
    mn = inp["rwkv_mu_next"][0]

    def put(blk, v):
        pp[:, blk * 32:(blk + 1) * 32] = _cols(v)

    put(PB_MUP_R, mp[0:E]); put(PB_MUP_K, mp[E:2 * E]); put(PB_MUP_V, mp[2 * E:3 * E])
    put(PB_MUN_R, mn[0:E]); put(PB_MUN_K, mn[E:2 * E]); put(PB_MUN_V, mn[2 * E:3 * E])
    put(PB_W0_0, inp["rwkv_w0"][0, 0]); put(PB_W0_1, inp["rwkv_w0"][0, 1])
    put(PB_A0_0, inp["rwkv_a0"][0, 0]); put(PB_A0_1, inp["rwkv_a0"][0, 1])
    put(PB_KK, inp["rwkv_k_k"][0]); put(PB_KA, inp["rwkv_k_a"][0])
    put(PB_RK, inp["rwkv_r_k"][0].reshape(-1))
    put(PB_LNW, inp["rwkv_ln_w"][0]); put(PB_LNB, inp["rwkv_ln_b"][0])
    put(PB_GLG, inp["gmlp_ln_g"][0]); put(PB_GLB, inp["gmlp_ln_b"][0])
    for j in range(4):
        pp[0:96, PL_MUP + j] = mp[3 * E + 96 * j:3 * E + 96 * (j + 1)]
        pp[0:96, PL_MUN + j] = mn[3 * E + 96 * j:3 * E + 96 * (j + 1)]
    rowp = np.concatenate([
        np.broadcast_to(inp["norm_g"][0][None, :], (128, D)),
        np.broadcast_to(inp["norm_g"][1][None, :], (128, D)),
        np.broadcast_to(inp["final_norm_g"][None, :], (128, D))], axis=1)
    gbs = np.broadcast_to(inp["gmlp_b_s"][0].reshape(1, -1), (128, 16 * 128))
    return pp, np.ascontiguousarray(rowp, np.float32), np.ascontiguousarray(gbs, np.float32)


def build(dbg=None, pairs=None, phases="ASBC", nmt1=2):
    dbg = dbg or {}
    pairs = list(range(NPAIR)) if pairs is None else list(pairs)
    nc = bass.Bass("TRN2", target_bir_lowering=False)
    P = Prog()
    ar = Arena(nc)

    def dram(name, shape, dt, kind):
        return nc.dram_tensor(name, list(shape), dt, kind=kind).ap()

    x_d = dram("x", [T, D], F32, "ExternalInput")
    pp_d = dram("pp", [128, NPP], F32, "ExternalInput")
    rowp_d = dram("rowp", [128, 3 * D], F32, "ExternalInput")
    cb_d = dram("cb", [128, CB_N], F32, "ExternalInput")
    cf_d = dram("cf", [128, CF_N], F32, "ExternalInput")
    rwin_d = dram("rw_in", [D, RIN], F32, "ExternalInput")
    rwup_d = dram("rw_up", [2, 96, E], F32, "ExternalInput")
    raup_d = dram("ra_up", [2, 96, E], F32, "ExternalInput")
    rwout_d = dram("rw_out", [E, D], F32, "ExternalInput")
    gwin_d = dram("gw_in", [D, 3 * E], F32, "ExternalInput")
    gwsT_d = dram("gw_sT", [16, 128, 128], F32, "ExternalInput")
    gbs_d = dram("gbs", [128, 16 * 128], F32, "ExternalInput")
    gwout_d = dram("gw_out", [E, D], F32, "ExternalInput")
    y_d = dram("y", [T, D], F32, "ExternalOutput")
    zs_d = dram("zs_scr", [96, 128, T], F32, "Internal")
    sg_d = dram("sg_scr", [NPAIR, 128, T], BF16, "Internal")
    oT_d = dram("oT_scr", [NPAIR, 128, T], BF16, "Internal")
    h1_d = dram("h1_scr", [T, D], F32, "Internal")
    dbg_out = {}

    def MM(out, lhsT, rhs, start=True, stop=True, r=(), w=()):
        P.emit("pe", lambda e: e.matmul(out, lhsT, rhs, start=start, stop=stop), reads=r, writes=w)

    def ACT(out, in_, func, bias=0.0, scale=1.0, r=(), w=(), accum=None):
        if accum is None:
            P.emit("act", lambda e: e.activation(out=out, in_=in_, func=func, bias=bias, scale=scale), reads=r, writes=w)
        else:
            P.emit("act", lambda e: e.activation(out=out, in_=in_, func=func, bias=bias, scale=scale, accum_out=accum), reads=r, writes=w)

    def TTo(eng, out, in0, in1, op, r=(), w=()):
        P.emit(eng, lambda e: e.tensor_tensor(out=out, in0=in0, in1=in1, op=op), reads=r, writes=w)

    def TS(eng, out, in0, s1, s2, op0, op1=None, r=(), w=()):
        if op1 is None:
            P.emit(eng, lambda e: e.tensor_scalar(out=out, in0=in0, scalar1=s1, scalar2=None, op0=op0), reads=r, writes=w)
        else:
            P.emit(eng, lambda e: e.tensor_scalar(out=out, in0=in0, scalar1=s1, scalar2=s2, op0=op0, op1=op1), reads=r, writes=w)

    def STT(out, in0, scalar, in1, op0, op1, r=(), w=()):
        P.emit("dve", lambda e: e.scalar_tensor_tensor(out=out, in0=in0, scalar=scalar, in1=in1, op0=op0, op1=op1), reads=r, writes=w)

    def CP(eng, out, in_, r=(), w=()):
        if eng == "act":
            ACT(out, in_, AF.Copy, r=r, w=w)
        else:
            P.emit(eng, lambda e: e.tensor_copy(out=out, in_=in_), reads=r, writes=w)

    def DMA(eng, out, in_, sem, r=(), w=()):
        P.emit(eng, lambda e: e.dma_start(out=out, in_=in_), reads=r, writes=w, dma=sem)

    def MEMSET(eng, ap, val, w=()):
        P.emit(eng, lambda e: e.memset(ap, val), writes=w)

    def dump(name, ap, shape, r, dt=F32):
        if name not in dbg:
            return
        d = dram("dbg_" + name, shape, dt, "ExternalOutput")
        dbg_out[name] = d
        DMA("sp", d, ap, "dbg_" + name, r=r, w=["dbg_" + name])
        P.emit("sp", lambda e: e.nop(), reads=["dbg_" + name])

    psb = [nc.alloc_psum_tensor("psb%d" % i, [128, 512], F32) for i in range(8)]

    cb = ar.alloc("cb", [128, CB_N], BF16)
    cf = ar.alloc("cf", [128, CF_N], F32)
    pp = ar.alloc("pp", [128, NPP], F32)
    DMA("pool", cb[:], cb_d, "cst0", w=["cb"])
    DMA("sp", cf[:], cf_d, "cst1", w=["cf"])
    DMA("sp", pp[:], pp_d, "cst2", w=["pp"])
    ident = cb[:, CB_IDENT:CB_IDENT + 128]
    fold = cb[:, CB_FOLD:CB_FOLD + 64]
    bones = cb[:, CB_BONES:CB_BONES + 128]
    resetm = cb[:, CB_RESET:CB_RESET + 2048]

    def pcol(blk, c):
        return pp[:, blk * 32 + c:blk * 32 + c + 1]

    for j in range(3):
        a = pp[:, (PB_MUP_R + j) * 32:(PB_MUP_R + j + 1) * 32]
        b = pp[:, (PB_MUN_R + j) * 32:(PB_MUN_R + j + 1) * 32]
        c = pp[:, (PB_C0_R + j) * 32:(PB_C0_R + j + 1) * 32]
        TTo("dve", c, a, b, ALU.add, r=["pp"], w=["pp"])
        TS("dve", c, c, -1.0, 1.0, ALU.mult, ALU.add, r=["pp"], w=["pp"])
    TTo("dve", pp[:, PL_C0:PL_C0 + 4], pp[:, PL_MUP:PL_MUP + 4], pp[:, PL_MUN:PL_MUN + 4], ALU.add, r=["pp"], w=["pp"])
    TS("dve", pp[:, PL_C0:PL_C0 + 4], pp[:, PL_C0:PL_C0 + 4], -1.0, 1.0, ALU.mult, ALU.add, r=["pp"], w=["pp"])
    TS("dve", pp[:, PB_OMKA * 32:(PB_OMKA + 1) * 32], pp[:, PB_KA * 32:(PB_KA + 1) * 32], -1.0, 1.0, ALU.mult, ALU.add, r=["pp"], w=["pp"])

    gmark0 = ar.mark()
    tdw = ar.alloc("tdw", [128, 4, T], BF16)
    gmark = ar.mark()

    def norm_transpose(src_rows, ntile, grow, hnT, col0, bufs, tag, pre=None):
        xt0, sq, hn, st = bufs
        for mt in range(ntile):
            s = mt % 2
            if pre is None:
                xt = xt0
                xkey = "%s_xt%d" % (tag, s)
                DMA("sp", xt[s][:], src_rows(mt), "%s_x%d" % (tag, s), w=[xkey])
            else:
                xt = {s: pre[mt][0]}
                xkey = pre[mt][1]
            MEMSET("pool", st[:, 0:1], 0.0, w=[tag + "_ss"])
            ACT(sq[:], xt[s][:], AF.Square, r=[xkey, tag + "_ss"], w=[tag + "_sq", tag + "_ss"], accum=st[:, 0:1])
            ACT(st[:, 1:2], st[:, 0:1], AF.Sqrt, bias=RMS_EPS, scale=1.0 / D, r=[tag + "_ss"], w=[tag + "_sd"])
            P.emit("dve", lambda e: e.reciprocal(out=st[:, 2:3], in_=st[:, 1:2]), reads=[tag + "_sd"], writes=[tag + "_rs"])
            STT(hn[s][:], xt[s][:], st[:, 2:3], grow, ALU.mult, ALU.mult,
                r=[xkey, tag + "_rs", "grow"], w=["%s_hn%d" % (tag, s)])
            for q in range(4):
                pb = psb[q % 2]
                for j in range(4):
                    kc = q * 4 + j
                    MM(pb[:, j * 128:(j + 1) * 128], hn[s][:, kc * 128:(kc + 1) * 128], ident,
                       r=["%s_hn%d" % (tag, s), "cb"], w=["psb%d" % (q % 2)])
                dst = hnT[:, q * 4:(q + 1) * 4, col0 + mt * 128:col0 + (mt + 1) * 128]
                src = pb[:, :].rearrange("p (j t) -> p j t", j=4)
                if q % 2 == 0:
                    CP("act", dst, src, r=["psb%d" % (q % 2)], w=["hnT"])
                else:
                    CP("dve", dst, src, r=["psb%d" % (q % 2)], w=["hnT"])

    if "A" in phases:
        hnT = ar.alloc("hnT", [128, 16, T + 2], BF16)
        grow = ar.alloc("grow", [128, D], F32)
        DMA("sp", grow[:], rowp_d[:, 0:D], "grow", w=["grow"])
        MEMSET("pool", hnT[:, :, 0:1], 0.0, w=["hnT"])
        MEMSET("pool", hnT[:, :, T + 1:T + 2], 0.0, w=["hnT"])
        m0 = ar.mark()
        xt = [ar.alloc("xt", [128, D], F32) for _ in range(2)]
        sq = ar.alloc("sq", [128, D], BF16)
        hn = [ar.alloc("hn", [128, D], BF16) for _ in range(2)]
        st = ar.alloc("st", [128, 4], F32)
        norm_transpose(lambda mt: x_d[mt * 128:(mt + 1) * 128, :], 16, grow[:], hnT, 1, (xt, sq, hn, st), "n0")
        ar.reset(m0, P)
        dump("hnT", hnT[:, 0, :], [128, T + 2], ["hnT"], BF16)

        NW = 3
        wt = [ar.alloc("wt", [128, 16, 128], BF16) for _ in range(NW)]
        zraw = [ar.alloc("zraw", [128, T + 2], F32) for _ in range(2)]
        zsb = [ar.alloc("zsb", [128, T], F32) for _ in range(2)]
        sgb = [ar.alloc("sgb", [128, T], BF16) for _ in range(2)]
        rwin_v = rwin_d.rearrange("(kc kk) n -> kk kc n", kk=128)
        chunks = []
        need = set()
        for p_ in pairs:
            need.update([p_, 32 + p_, 64 + p_])
        for j in range(4):
            chunks.append(("lora", 3 * E + 96 * j, 96, j))
        for cc in sorted(need):
            chunks.append(("rkv", cc * 128, 128, cc))
        for p_ in pairs:
            chunks.append(("gate", SHIFT + p_ * 128, 128, p_))

        def load_w(i):
            kind, c0_, M, info = chunks[i]
            s = i % NW
            DMA("pool", wt[s][:, :, 0:M], rwin_v[:, :, c0_:c0_ + M], "wtA%d" % s, w=["wtA%d" % s])

        for i in range(min(2, len(chunks))):
            load_w(i)
        for i, (kind, c0_, M, info) in enumerate(chunks):
            s = i % NW
            zr = zraw[i % 2]
            zk = "zraw%d" % (i % 2)
            if i + 2 < len(chunks):
                load_w(i + 2)
            for blk in range(5):
                pb = psb[blk % 2]
                for kc in range(16):
                    MM(pb[0:M, 0:410], wt[s][:, kc, 0:M], hnT[:, kc, blk * 410:(blk + 1) * 410],
                       start=(kc == 0), stop=(kc == 15), r=["wtA%d" % s, "hnT"], w=["psb%d" % (blk % 2)])
                CP("act" if blk % 2 == 0 else "dve", zr[0:M, blk * 410:(blk + 1) * 410], pb[0:M, 0:410],
                   r=["psb%d" % (blk % 2)], w=["%s_%d" % (zk, blk)])
            zr_all = ["%s_%d" % (zk, b) for b in range(5)]
            if kind == "gate":
                so = sgb[i % 2]
                ACT(so[:], zr[:, 1:T + 1], AF.Silu, r=zr_all, w=["sgb%d" % (i % 2)])
                DMA("sp", sg_d[info], so[:], "sgst%d" % (i % 2), r=["sgb%d" % (i % 2)], w=["sg_d%d" % info])
                continue
            if kind == "rkv":
                q, c = info // 32, info % 32
                c0c, mpc, mnc = pcol(PB_C0_R + q, c), pcol(PB_MUP_R + q, c), pcol(PB_MUN_R + q, c)
            else:
                c0c, mpc, mnc = pp[0:96, PL_C0 + info:PL_C0 + info + 1], pp[0:96, PL_MUP + info:PL_MUP + info + 1], pp[0:96, PL_MUN + info:PL_MUN + info + 1]
            zo = zsb[i % 2]
            zok = "zsb%d" % (i % 2)
            ACT(zo[0:M, :], zr[0:M, 1:T + 1], AF.Copy, scale=c0c, r=zr_all + ["pp"], w=[zok])
            STT(zo[0:M, :], zr[0:M, 0:T], mpc, zo[0:M, :], ALU.mult, ALU.add, r=zr_all + ["pp", zok], w=[zok])
            STT(zo[0:M, :], zr[0:M, 2:T + 2], mnc, zo[0:M, :], ALU.mult, ALU.add, r=zr_all + ["pp", zok], w=[zok])
            if kind == "rkv":
                DMA("sp", zs_d[info], zo[:], "zsst%d" % (i % 2), r=[zok], w=["zs_d%d" % info])
            else:
                if info < 2:
                    ACT(tdw[0:96, info, :], zo[0:96, :], AF.Tanh, r=[zok], w=["tdw%d" % info])
                else:
                    CP("act", tdw[0:96, info, :], zo[0:96, :], r=[zok], w=["tdw%d" % info])
        ar.reset(gmark, P)

    if "S" in phases:
        def v3(ap):
            return ap.rearrange("p (c t) -> p c t", t=CH)

        def bc4(ap):
            return v3(ap)[:, :, None, :].broadcast_to([128, NCHK, 2, CH])

        def flat(t5):
            return t5[:].rearrange("p c q h t -> p (c q h t)")

        rwup_v = [rwup_d[d] for d in range(2)]
        raup_v = [raup_d[d] for d in range(2)]
        r_f = ar.alloc("r_f", [128, T], F32)
        k_f = ar.alloc("k_f", [128, T], F32)
        v_f = ar.alloc("v_f", [128, T], F32)
        sg = ar.alloc("sg", [128, T], BF16)
        kk = ar.alloc("kk", [128, T], F32)
        ksum = ar.alloc("ksum", [128, T], BF16)
        Yb = ar.alloc("Yb", [128, NCHK, CH], F32)
        Vst = ar.alloc("Vst", [128, NCHK, CH], BF16)
        wup = ar.alloc("wup", [128, 2, 2, 128], BF16)
        t_sw = ar.alloc("t_sw", [128, T], F32)
        t_P = ar.alloc("t_P", [128, T], F32)
        t_a = ar.alloc("t_a", [128, T], F32)
        t_kd = ar.alloc("t_kd", [128, T], F32)
        t_b = ar.alloc("t_b", [128, T], F32)
        ex0_ = ar.alloc("ex", [128, NCHK, 2, CH], F32)
        ex = [ex0_, ex0_]
        ar_bd = ar.alloc("ar_bd", [128, NCHK, 2, 2, CH], BF16)
        bk_bd = ar.alloc("bk_bd", [128, NCHK, 2, 2, CH], BF16)
        bkw_bd = ar.alloc("bkw_bd", [128, NCHK, 2, 2, CH], BF16)
        tot = ar.alloc("tot", [128, NCHK], F32)
        WC = ar.alloc("WC", [128, NCHK], F32)
        gst = ar.alloc("gst", [128, 6, NCHK], F32)
        S_b = ar.alloc("S_b", [128, CH], BF16)
        S_f2 = ar.alloc("S_f2", [128, CH], F32)
        GB = 4
        Amat4 = [ar.alloc("Amat4", [128, GB, 512], BF16) for _ in range(2)]
        Lm4 = ar.alloc("Lm4", [128, GB, 128], BF16)
        LGL = [ar.alloc("LGL", [128, GB, 128], BF16) for _ in range(2)]
        LGG = [ar.alloc("LGG", [128, GB, 128], BF16) for _ in range(2)]
        T4 = [ar.alloc("T4", [128, GB, 128], BF16) for _ in range(2)]
        BK4 = [ar.alloc("BK4", [128, GB, 256], BF16) for _ in range(2)]
        Xb = ar.alloc("Xb", [128, CH], BF16)
        Ub = ar.alloc("Ub", [128, CH], BF16)
        ynx = flat(ar_bd)[:, 0:2 * T].rearrange("p (c h t) -> p c h t", h=2, t=CH)
        negm = [cf[:, CF_NEG + h:CF_NEG + h + 1] for h in range(2)]
        m01b = cb[:, CB_M01:CB_M01 + 2]


        for p_ in pairs:
            DMA("sp", r_f[:], zs_d[p_], "ld_r", w=["r_f"])
            DMA("sp", k_f[:], zs_d[32 + p_], "ld_k", w=["k_f"])
            DMA("sp", v_f[:], zs_d[64 + p_], "ld_v", w=["v_f"])
            DMA("sp", sg[:], sg_d[p_], "ld_sg", w=["sg"])
            for d in range(2):
                DMA("pool", wup[0:96, 0, d, :], rwup_v[d][:, p_ * 128:(p_ + 1) * 128], "ld_wu%d" % d, w=["wup0%d" % d])
                DMA("pool", wup[0:96, 1, d, :], raup_v[d][:, p_ * 128:(p_ + 1) * 128], "ld_au%d" % d, w=["wup1%d" % d])
            sqt = flat(ar_bd)[:, 0:T]
            ACT(sqt, k_f[:], AF.Square, scale=pcol(PB_KK, p_), r=["k_f", "pp"], w=["ar_bd"])
            for blk in range(4):
                bs = slice(blk * 512, (blk + 1) * 512)
                MM(psb[2][:, :], bones, sqt[:, bs], r=["ar_bd", "cb"], w=["psb2"])
                ACT(t_P[:, bs], psb[2][:, :], AF.Sqrt, bias=1e-24, r=["psb2"], w=["t_P"])
            P.emit("dve", lambda e: e.reciprocal(out=t_P[:], in_=t_P[:]), reads=["t_P"], writes=["t_P"])
            STT(kk[:], k_f[:], pcol(PB_KK, p_), t_P[:], ALU.mult, ALU.mult, r=["k_f", "pp", "t_P"], w=["kk"])
            dump("kk", kk[:], [128, T], ["kk"])
            vbd = flat(bkw_bd)[:, 0:2 * T].rearrange("p (c h t) -> p c h t", h=2, t=CH)
            TTo("pool", vbd, bc4(v_f[:]), m01b[:, None, :, None].broadcast_to([128, NCHK, 2, CH]),
                ALU.mult, r=["v_f", "cb"], w=["bkw_bd"])
            for c8 in range(4):
                for j in range(8):
                    c = c8 * 8 + j
                    MM(psb[3][:, j * 64:(j + 1) * 64], vbd[:, c, :, :].rearrange("p h t -> p (h t)"), fold, r=["bkw_bd", "cb"], w=["psb3"])
                CP("act", Vst[:, c8 * 8:(c8 + 1) * 8, :], psb[3][:, :].rearrange("p (c v) -> p c v", v=CH), r=["psb3"], w=["Vst"])

            for d in range(2):
                sgn = -CDEC if d == 0 else CDEC
                for blk in range(4):
                    bs = slice(blk * 512, (blk + 1) * 512)
                    MM(psb[2][:, :], wup[0:96, 0, d, :], tdw[0:96, d, bs], r=["wup0%d" % d, "tdw%d" % d], w=["psb2"])
                    ACT(t_sw[:, bs], psb[2][:, :], AF.Sigmoid, bias=pcol(PB_W0_0 + d, p_), r=["psb2", "pp"], w=["t_sw"])
                    MM(psb[3][:, :], wup[0:96, 1, d, :], tdw[0:96, 2 + d, bs], r=["wup1%d" % d, "tdw%d" % (2 + d)], w=["psb3"])
                    ACT(t_a[:, bs], psb[3][:, :], AF.Sigmoid, bias=pcol(PB_A0_0 + d, p_), r=["psb3", "pp"], w=["t_a"])
                TS("pool", t_kd[:], t_a[:], pcol(PB_KA, p_), pcol(PB_OMKA, p_), ALU.mult, ALU.add, r=["t_a", "pp"], w=["t_kd"])
                TTo("pool", t_kd[:], t_kd[:], k_f[:], ALU.mult, r=["t_kd", "k_f"], w=["t_kd"])
                if d == 0:
                    CP("pool", ksum[:], t_kd[:], r=["t_kd"], w=["ksum"])
                else:
                    TTo("pool", ksum[:], ksum[:], t_kd[:], ALU.add, r=["ksum", "t_kd"], w=["ksum"])
                TTo("dve", t_b[:], kk[:], t_a[:], ALU.mult, r=["kk", "t_a"], w=["t_b"])
                P.emit("dve", lambda e: e.tensor_tensor_scan(out=t_P[:], data0=resetm, data1=t_sw[:], initial=0.0,
                                                             op0=ALU.mult, op1=ALU.add),
                       reads=["cb", "t_sw"], writes=["t_P"])
                CP("dve", tot[:, :, None], v3(t_P[:])[:, :, CH - 1:CH], r=["t_P"], w=["tot"])
                ACT(WC[:], tot[:], AF.Exp, scale=-CDEC, r=["tot"], w=["WC"])
                TTo("pool", t_sw[:], t_P[:], t_sw[:], ALU.subtract, r=["t_P", "t_sw"], w=["t_sw"])
                totb = tot[:, :, None].broadcast_to([128, NCHK, CH])
                if d == 0:
                    Lr, Lx = t_P, t_sw
                else:
                    TTo("pool", v3(t_sw[:]), v3(t_sw[:]), totb, ALU.subtract, r=["t_sw", "tot"], w=["t_sw"])
                    TTo("dve", v3(t_P[:]), v3(t_P[:]), totb, ALU.subtract, r=["t_P", "tot"], w=["t_P"])
                    Lr, Lx = t_sw, t_P
                lrk = "t_P" if Lr is t_P else "t_sw"
                lxk = "t_P" if Lx is t_P else "t_sw"
                for h in range(2):
                    ACT(ex[0][:, :, h, :], v3(Lr[:]), AF.Exp, bias=negm[h], scale=sgn, r=[lrk, "cf"], w=["ex0_%d" % h])
                TTo("dve", ar_bd[:, :, 1, :, :], bc4(r_f[:]), ex[0][:], ALU.mult,
                    r=["r_f", "ex0_0", "ex0_1"], w=["ar_bd"])
                for h in range(2):
                    ACT(ex[1][:, :, h, :], v3(Lx[:]), AF.Exp, bias=negm[h], scale=sgn, r=[lxk, "cf"], w=["ex0_%d" % h])
                TTo("pool", ar_bd[:, :, 0, :, :], bc4(kk[:]), ex[1][:], ALU.mult,
                    r=["kk", "ex0_0", "ex0_1"], w=["ar_bd"])
                for h in range(2):
                    ACT(ex[0][:, :, h, :], v3(Lr[:]), AF.Exp, bias=negm[h], scale=-sgn, r=[lrk, "cf"], w=["ex0_%d" % h])
                TTo("dve", bk_bd[:, :, 0, :, :], bc4(t_b[:]), ex[0][:], ALU.mult,
                    r=["t_b", "ex0_0", "ex0_1"], w=["bk_bd"])
                TTo("pool", bk_bd[:, :, 1, :, :], bc4(t_kd[:]), ex[0][:], ALU.mult,
                    r=["t_kd", "ex0_0", "ex0_1"], w=["bk_bd"])
                wcb = WC[:, :, None, None].broadcast_to([128, NCHK, 2, CH])
                for q in range(2):
                    TTo("dve" if q == 0 else "pool", bkw_bd[:, :, q, :, :], bk_bd[:, :, q, :, :], wcb, ALU.mult,
                        r=["bk_bd", "WC"], w=["bkw_bd"])
                if p_ == pairs[0]:
                    dump("rt%d" % d, ar_bd[:, :, 1, :, :], [128, NCHK, 2, CH], ["ar_bd"], BF16)
                    dump("at%d" % d, ar_bd[:, :, 0, :, :], [128, NCHK, 2, CH], ["ar_bd"], BF16)
                MEMSET("pool", S_b[:], 0.0, w=["S_b"])
                MEMSET("dve", S_f2[:], 0.0, w=["S_f2"])
                m4 = cb[:, CB_M4 + d * 512:CB_M4 + (d + 1) * 512]
                mL = cb[:, CB_ML + d * 128:CB_ML + (d + 1) * 128]
                sgn2 = cb[:, None, CB_SGN:CB_SGN + 256].broadcast_to([128, 2, 256])
                corder = list(range(NCHK)) if d == 0 else list(range(NCHK - 1, -1, -1))
                NG = NCHK // GB

                def chv(t5, c, q):
                    return t5[:, c, q, :, :].rearrange("p h t -> p (h t)")

                def pre_ops(g):
                    gp = g % 2
                    cl = corder[g * GB:(g + 1) * GB]
                    A4, Tg, BKg = Amat4[gp], T4[gp], BK4[gp]
                    kA, kT, kBK = "Amat4_%d" % gp, "T4_%d" % gp, "BK4_%d" % gp
                    ops = []
                    for j, c in enumerate(cl):
                        bk = 4 + j % 2
                        ar_c = ar_bd[:, c, :, :, :].rearrange("p q h t -> p (q h t)")
                        ops.append(lambda c=c, bk=bk, ar_c=ar_c: MM(psb[bk][:, 0:256], chv(bk_bd, c, 0), ar_c, r=["bk_bd", "ar_bd"], w=["psb%d" % bk]))
                        ops.append(lambda c=c, bk=bk, ar_c=ar_c: MM(psb[bk][:, 256:512], chv(bk_bd, c, 1), ar_c, r=["bk_bd", "ar_bd"], w=["psb%d" % bk]))
                        ops.append(lambda j=j, bk=bk: TTo("dve", A4[:, j, :], psb[bk][:, :], m4, ALU.mult, r=["psb%d" % bk, "cb"], w=[kA + "_%d" % j]))
                    for j, c in enumerate(cl):
                        ops.append(lambda j=j, c=c: MM(psb[6][:, j * 128:(j + 1) * 128], chv(ar_bd, c, 0), chv(bk_bd, c, 0), r=["ar_bd", "bk_bd"], w=["psb6"]))
                    ops.append(lambda: TTo("dve", Lm4[:], psb[6][:, :].rearrange("p (j t) -> p j t", j=GB),
                                           mL[:, None, :].broadcast_to([128, GB, 128]), ALU.mult, r=["psb6", "cb"], w=["Lm4"]))
                    akeys = [kA + "_%d" % j for j in range(GB)]
                    ops.append(lambda: TTo("pool", Tg[:], A4[:, :, 0:128], ident[:, None, :].broadcast_to([128, GB, 128]), ALU.add,
                                           r=akeys + ["cb"], w=[kT]))
                    for it in range(5):
                        if it == 0:
                            Lc, Gc, lck = Lm4, A4, ["Lm4"] + akeys
                        else:
                            Lc, Gc, lck = LGL[(it - 1) % 2], LGG[(it - 1) % 2], ["LGL%d" % ((it - 1) % 2), "LGG%d" % ((it - 1) % 2)]
                        Ln, Gn = LGL[it % 2], LGG[it % 2]
                        for j in range(GB):
                            ops.append(lambda j=j, Lc=Lc, Gc=Gc, lck=lck: MM(psb[5][:, j * 128:(j + 1) * 128], Gc[:, j, 0:128], Lc[:, j, 0:128], r=lck, w=["psb5"]))
                        ops.append(lambda Ln=Ln, it=it: CP("act", Ln[:], psb[5][:, :].rearrange("p (j t) -> p j t", j=GB), r=["psb5"], w=["LGL%d" % (it % 2)]))
                        if it < 4:
                            for j in range(GB):
                                ops.append(lambda j=j, Lc=Lc, Gc=Gc, lck=lck: MM(psb[2][:, j * 128:(j + 1) * 128], Lc[:, j, 0:128], Gc[:, j, 0:128], r=lck, w=["psb2"]))
                            ops.append(lambda Gn=Gn, it=it: CP("act", Gn[:], psb[2][:, :].rearrange("p (j t) -> p j t", j=GB), r=["psb2"], w=["LGG%d" % (it % 2)]))
                        for j in range(GB):
                            ops.append(lambda j=j, Ln=Ln, it=it: MM(psb[3][:, j * 128:(j + 1) * 128], Ln[:, j, :], Tg[:, j, :], r=["LGL%d" % (it % 2), kT], w=["psb3"]))
                        ops.append(lambda: TTo("dve", Tg[:], Tg[:], psb[3][:, :].rearrange("p (j t) -> p j t", j=GB), ALU.add, r=[kT, "psb3"], w=[kT]))
                    for half in range(2):
                        bk = half
                        for jj in range(2):
                            c = cl[half * 2 + jj]
                            ops.append(lambda c=c, jj=jj, bk=bk: MM(psb[bk][:, jj * 256:jj * 256 + 128], chv(bkw_bd, c, 0), ident, r=["bkw_bd", "cb"], w=["psb%d" % bk]))
                            ops.append(lambda c=c, jj=jj, bk=bk: MM(psb[bk][:, jj * 256 + 128:jj * 256 + 256], chv(bkw_bd, c, 1), ident, r=["bkw_bd", "cb"], w=["psb%d" % bk]))
                        ops.append(lambda half=half, bk=bk: TTo("dve", BKg[:, half * 2:half * 2 + 2, :], psb[bk][:, :].rearrange("p (j t) -> p j t", j=2),
                                                                sgn2, ALU.mult, r=["psb%d" % bk, "cb"], w=[kBK + "_%d" % half]))
                    return ops

                def chain_ops(g):
                    gp = g % 2
                    cl = corder[g * GB:(g + 1) * GB]
                    A4, Tg, BKg = Amat4[gp], T4[gp], BK4[gp]
                    kT = "T4_%d" % gp
                    ops = []
                    for j, c in enumerate(cl):
                        kAj = "Amat4_%d_%d" % (gp, j)
                        kBKj = "BK4_%d_%d" % (gp, j // 2)
                        ops.append(lambda c=c: MM(psb[7][:, 0:64], chv(ar_bd, c, 0), S_b[:], start=True, stop=False, r=["ar_bd", "S_b"], w=["psb7"]))
                        ops.append(lambda c=c, j=j, kAj=kAj: MM(psb[7][:, 0:64], A4[:, j, 256:384], Vst[:, c, :], start=False, stop=True, r=[kAj, "Vst"], w=["psb7"]))
                        ops.append(lambda: CP("act", Xb[:], psb[7][:, 0:64], r=["psb7"], w=["Xb"]))
                        ops.append(lambda j=j: MM(psb[7][:, 64:128], Tg[:, j, :], Xb[:], r=[kT, "Xb"], w=["psb7"]))
                        ops.append(lambda: CP("act", Ub[:], psb[7][:, 64:128], r=["psb7"], w=["Ub"]))
                        import os as _os2
                        _v = "a"
                        sgrp = [
                            lambda j=j, kBKj=kBKj: MM(psb[7][:, 192:256], BKg[:, j, 0:128], Ub[:], start=True, stop=False, r=[kBKj, "Ub"], w=["psb7"]),
                            lambda j=j, c=c, kBKj=kBKj: MM(psb[7][:, 192:256], BKg[:, j, 128:256], Vst[:, c, :], start=False, stop=True, r=[kBKj, "Vst"], w=["psb7"]),
                        ]
                        ygrp = [
                            lambda c=c: MM(psb[7][:, 128:192], chv(ar_bd, c, 1), S_b[:], start=True, stop=False, r=["ar_bd", "S_b"], w=["psb7"]),
                            lambda j=j, kAj=kAj: MM(psb[7][:, 128:192], A4[:, j, 128:256], Ub[:], start=False, stop=False, r=[kAj, "Ub"], w=["psb7"]),
                            lambda j=j, c=c, kAj=kAj: MM(psb[7][:, 128:192], A4[:, j, 384:512], Vst[:, c, :], start=False, stop=True, r=[kAj, "Vst"], w=["psb7"]),
                        ]
                        if "b" in _v:
                            supd = [lambda c=c: STT(S_f2[:], S_f2[:], WC[:, c:c + 1], psb[7][:, 192:256], ALU.mult, ALU.add, r=["S_f2", "WC", "psb7"], w=["S_f2"]),
                                    lambda: CP("act", S_b[:], S_f2[:], r=["S_f2"], w=["S_b"])]
                        else:
                            supd = [lambda c=c: STT(S_b[:], S_b[:], WC[:, c:c + 1], psb[7][:, 192:256], ALU.mult, ALU.add, r=["S_b", "WC", "psb7"], w=["S_b"])]
                        if d == 0:
                            yev = [lambda c=c: CP("act", Yb[:, c, :], psb[7][:, 128:192], r=["psb7"], w=["Yb%d" % c])]
                        else:
                            yev = [lambda c=c: TTo("dve", Yb[:, c, :], Yb[:, c, :], psb[7][:, 128:192], ALU.add, r=["psb7", "Yb%d" % c], w=["Yb%d" % c])]
                        if "a" in _v:
                            ops.extend(ygrp + yev + sgrp + supd)
                        else:
                            ops.extend(sgrp + ygrp + supd + yev)
                    return ops

                def merged(a, b):
                    if not MERGE:
                        for f in a:
                            f()
                        for f in b:
                            f()
                        return
                    na, nb = len(a), len(b)
                    ia = ib = 0
                    while ia < na or ib < nb:
                        if ib >= nb or (ia < na and ia * nb <= ib * na):
                            a[ia](); ia += 1
                        else:
                            b[ib](); ib += 1

                import os as _os
                _sk = ""
                for op_ in ([] if "pre" in _sk else pre_ops(0)):
                    op_()
                for g in range(NG):
                    merged([] if "chain" in _sk else chain_ops(g), pre_ops(g + 1) if (g + 1 < NG and "pre" not in _sk) else [])
            ykeys = ["Yb%d" % c for c in range(NCHK)]
            if p_ == pairs[0]:
                dump("Y", Yb[:], [128, NCHK, CH], ykeys)
            s1, s2_, mean, msq, var, rstd = [gst[:, j, :] for j in range(6)]
            P.emit("dve", lambda e: e.tensor_reduce(out=s1, in_=Yb[:], axis=AX.X, op=ALU.add), reads=ykeys, writes=["gst0"])
            sqf = ex[0][:].rearrange("p c h t -> p (c h t)")[:, 0:T]
            TTo("pool", sqf, Yb[:].rearrange("p c v -> p (c v)"), Yb[:].rearrange("p c v -> p (c v)"), ALU.mult,
                r=ykeys + ["ex0_0", "ex0_1"], w=["ex0_0", "ex0_1"])
            P.emit("dve", lambda e: e.tensor_reduce(out=s2_, in_=v3(sqf), axis=AX.X, op=ALU.add), reads=["ex0_0", "ex0_1"], writes=["gst1"])
            TS("dve", mean, s1, 1.0 / CH, None, ALU.mult, r=["gst0"], w=["gst2"])
            TTo("dve", msq, mean, mean, ALU.mult, r=["gst2"], w=["gst3"])
            STT(var, s2_, 1.0 / CH, msq, ALU.mult, ALU.subtract, r=["gst1", "gst3"], w=["gst4"])
            ACT(var, var, AF.Sqrt, bias=GN_EPS, r=["gst4"], w=["gst4"])
            P.emit("dve", lambda e: e.reciprocal(out=rstd, in_=var), reads=["gst4"], writes=["gst5"])
            TTo("dve", Yb[:], Yb[:], mean[:, :, None].broadcast_to([128, NCHK, CH]), ALU.subtract, r=ykeys + ["gst2"], w=ykeys)
            MEMSET("pool", ynx, 0.0, w=["ar_bd"])
            for h in range(2):
                hs = slice(h * 64, (h + 1) * 64)
                TTo("dve" if h == 0 else "pool", ynx[hs, :, h, :], Yb[hs, :, :], rstd[hs, :, None].broadcast_to([64, NCHK, CH]), ALU.mult,
                    r=ykeys + ["gst5"], w=["ar_bd"])
            o1 = t_a
            for c8 in range(4):
                for j in range(8):
                    c = c8 * 8 + j
                    MM(psb[3][:, j * 64:(j + 1) * 64], ynx[:, c, :, :].rearrange("p h t -> p (h t)"), fold, r=["ar_bd", "cb"], w=["psb3"])
                ACT(o1[:, c8 * 512:(c8 + 1) * 512], psb[3][:, :], AF.Identity, bias=pcol(PB_LNB, p_), scale=pcol(PB_LNW, p_),
                    r=["psb3", "pp"], w=["t_a"])
            TTo("pool", ksum[:], ksum[:], r_f[:], ALU.mult, r=["ksum", "r_f"], w=["ksum"])
            q2 = flat(bk_bd)[:, 0:T]
            TS("dve", q2, ksum[:], pcol(PB_RK, p_), None, ALU.mult, r=["ksum", "pp"], w=["bk_bd"])
            for blk in range(4):
                bs = slice(blk * 512, (blk + 1) * 512)
                MM(psb[2][:, :], bones, q2[:, bs], r=["bk_bd", "cb"], w=["psb2"])
                TTo("dve", t_b[:, bs], psb[2][:, :], v_f[:, bs], ALU.mult, r=["psb2", "v_f"], w=["t_b"])
            TTo("pool", o1[:], o1[:], t_b[:], ALU.add, r=["t_a", "t_b"], w=["t_a"])
            oTb = flat(bkw_bd)[:, 2 * T:3 * T]
            TTo("pool", oTb, o1[:], sg[:], ALU.mult, r=["t_a", "sg"], w=["bkw_bd"])
            DMA("sp", oT_d[p_], oTb, "st_oT", r=["bkw_bd"], w=["oT_d%d" % p_])
            if p_ == pairs[0]:
                dump("oT", oTb, [128, T], ["bkw_bd"], BF16)
        ar.reset(gmark, P)

    def out_proj(src_oT, wout_dram, res_rows, final, tag, grow=None):
        m0 = ar.mark()
        oTt = ar.alloc("oTt", [128, NPAIR, 512], BF16)
        wo = [ar.alloc("wo", [128, NPAIR, 512], BF16) for _ in range(2)]
        xr = [ar.alloc("xr", [128, D], F32) for _ in range(4)]
        sq = ar.alloc("sqo", [128, D], BF16)
        st = ar.alloc("sto", [128, 4, 4], F32)
        wo_v = wout_dram.rearrange("(c p) n -> p c n", p=128)
        oT_v = src_oT.rearrange("c p t -> p c t")
        for tb in range(4):
            DMA("sp", oTt[:], oT_v[:, :, tb * 512:(tb + 1) * 512], tag + "_oT", r=["oT_all"], w=[tag + "oTt"])
            for mt in range(4):
                DMA("sp", xr[mt][:], res_rows(tb * 4 + mt), "%s_xr%d" % (tag, mt), r=["res_all"], w=["%sxr%d" % (tag, mt)])
            for nb in range(4):
                s = nb % 2
                DMA("pool", wo[s][:], wo_v[:, :, nb * 512:(nb + 1) * 512], "%s_wo%d" % (tag, s), w=["%swo%d" % (tag, s)])
                for mt in range(4):
                    po = psb[mt][:, :]
                    pk = "psb%d" % mt
                    for c in range(NPAIR):
                        MM(po, oTt[:, c, mt * 128:(mt + 1) * 128], wo[s][:, c, :], start=(c == 0), stop=(c == NPAIR - 1),
                           r=[tag + "oTt", "%swo%d" % (tag, s)], w=[pk])
                    TTo("dve", xr[mt][:, nb * 512:(nb + 1) * 512], po, xr[mt][:, nb * 512:(nb + 1) * 512], ALU.add,
                        r=[pk, "%sxr%d" % (tag, mt)], w=["%sxr%d" % (tag, mt)])
            for mt in range(4):
                rows = slice((tb * 4 + mt) * 128, (tb * 4 + mt + 1) * 128)
                xk = "%sxr%d" % (tag, mt)
                if not final:
                    DMA("sp", h1_d[rows, :], xr[mt][:], "%s_st%d" % (tag, mt), r=[xk], w=["h1_all"])
                else:
                    ACT(sq[:], xr[mt][:], AF.Square, r=[xk], w=[tag + "sq", tag + "ss%d" % mt], accum=st[:, mt, 0:1])
                    ACT(st[:, mt, 1:2], st[:, mt, 0:1], AF.Sqrt, bias=RMS_EPS, scale=1.0 / D, r=[tag + "ss%d" % mt], w=[tag + "sd%d" % mt])
                    P.emit("dve", lambda e, mt=mt: e.reciprocal(out=st[:, mt, 2:3], in_=st[:, mt, 1:2]),
                           reads=[tag + "sd%d" % mt], writes=[tag + "rs%d" % mt])
                    STT(xr[mt][:], xr[mt][:], st[:, mt, 2:3], grow, ALU.mult, ALU.mult, r=[xk, tag + "rs%d" % mt, "growf"], w=[xk])
                    DMA("sp", y_d[rows, :], xr[mt][:], "%s_st%d" % (tag, mt), r=[xk], w=["y_all"])
        ar.reset(m0, P)

    ar.reset(gmark0, P)
    if "B" in phases:
        P.emit("sp", lambda e: e.nop(), reads=["oT_d%d" % p_ for p_ in pairs], writes=["oT_all"])
        out_proj(oT_d, rwout_d, lambda mt: x_d[mt * 128:(mt + 1) * 128, :], False, "B")

    if "C" in phases:
        NMT = nmt1
        TT1 = NMT * 128
        NTL = T // TT1
        grow1 = ar.alloc("grow1", [128, D], F32)
        DMA("sp", grow1[:], rowp_d[:, D:2 * D], "grow1", w=["grow"])
        wsT = ar.alloc("wsT", [128, 16, 128], BF16)
        DMA("pool", wsT[:], gwsT_d.rearrange("g j i -> j g i"), "wsT", w=["wsT"])
        biasf = ar.alloc("biasf", [128, NPAIR, 128], F32)
        mC = ar.mark()
        bsb = ar.alloc("bsb", [128, 16, 128], F32)
        DMA("sp", bsb[:], gbs_d.rearrange("p (g i) -> p g i", i=128), "bsb", w=["bsb"])
        ones_b = ar.alloc("ones_b", [128, 128], BF16)
        MEMSET("pool", ones_b[:], 1.0, w=["ones_b"])
        for g in range(16):
            MM(psb[2][:, 0:128], ones_b[:], wsT[:, g, :], r=["ones_b", "wsT"], w=["psb2"])
            for j in range(2):
                fc = 2 * g + j
                STT(biasf[:, fc, :], psb[2][:, 0:128], pcol(PB_GLB, fc), bsb[:, g, :], ALU.mult, ALU.add,
                    r=["psb2", "pp", "bsb"], w=["biasf"])
        ar.reset(mC, P)
        hnT1 = ar.alloc("hnT1", [128, 16, TT1], BF16)
        sq = ar.alloc("sq1", [128, D], BF16)
        hn = [ar.alloc("hn1", [128, D], BF16) for _ in range(2)]
        st = ar.alloc("st1", [128, 4], F32)
        vh = [ar.alloc("vh", [128, E], BF16) for _ in range(NMT)]
        lst = ar.alloc("lst", [128, NMT, 24], F32)
        ug = ar.alloc("ug", [128, NPAIR, TT1], BF16)
        tu = [ar.alloc("tu", [128, TT1], F32) for _ in range(2)]
        tg = [ar.alloc("tg", [128, TT1], F32) for _ in range(2)]
        wv = [ar.alloc("wv", [128, 16, 256], BF16) for _ in range(2)]
        wug = [ar.alloc("wug", [128, 2, 16, 128], BF16) for _ in range(2)]
        wo2 = [ar.alloc("wo2", [128, NPAIR, 256], BF16) for _ in range(2)]
        h2 = [ar.alloc("h2", [128, D], F32) for _ in range(NMT)]
        growf = ar.alloc("growf", [128, D], F32)
        DMA("sp", growf[:], rowp_d[:, 2 * D:3 * D], "growf", w=["growf"])
        gwin_v = gwin_d.rearrange("(kc kk) n -> kk kc n", kk=128)
        gwo_v = gwout_d.rearrange("(c p) n -> p c n", p=128)
        for tl in range(NTL):
            t0 = tl * TT1
            for mt in range(NMT):
                DMA("sp", h2[mt][:], h1_d[t0 + mt * 128:t0 + (mt + 1) * 128, :], "ld_h2_%d" % mt, r=["h1_all"], w=["h2_%d" % mt])
            norm_transpose(None, NMT, grow1[:], hnT1, 0, (None, sq, hn, st), "n1",
                           pre=[(h2[mt], "h2_%d" % mt) for mt in range(NMT)])
            for mt in range(NMT):
                MEMSET("pool", lst[:, mt, 8:24], 0.0, w=["lsp%d" % mt])
            for vb in range(16):
                s = vb % 2
                DMA("pool", wv[s][:], gwin_v[:, :, E + vb * 256:E + (vb + 1) * 256], "wv%d" % s, w=["wv%d" % s])
                for mt in range(NMT):
                    pb = psb[mt % 2]
                    for kc in range(16):
                        MM(pb[:, 0:256], hnT1[:, kc, mt * 128:(mt + 1) * 128], wv[s][:, kc, :], start=(kc == 0), stop=(kc == 15),
                           r=["hnT", "wv%d" % s], w=["psb%d" % (mt % 2)])
                    ACT(vh[mt][:, vb * 256:(vb + 1) * 256], pb[:, 0:256], AF.Gelu_apprx_tanh, r=["psb%d" % (mt % 2), "lsp%d" % mt],
                        w=["vh%d" % mt, "lsp%d" % mt], accum=lst[:, mt, 8 + vb:9 + vb])
            for mt in range(NMT):
                P.emit("dve", lambda e, mt=mt: e.tensor_reduce(out=lst[:, mt, 0:1], in_=lst[:, mt, 8:24], axis=AX.X, op=ALU.add),
                       reads=["lsp%d" % mt], writes=["ls0_%d" % mt])
                MEMSET("pool", lst[:, mt, 1:2], 0.0, w=["ls1_%d" % mt])
                ACT(sq[:], vh[mt][:, 0:D], AF.Square, r=["vh%d" % mt, "ls1_%d" % mt], w=["n1_sq", "ls1_%d" % mt], accum=lst[:, mt, 1:2])
                MEMSET("pool", lst[:, mt, 6:7], 0.0, w=["ls6_%d" % mt])
                ACT(sq[:], vh[mt][:, D:E], AF.Square, r=["vh%d" % mt, "ls6_%d" % mt], w=["n1_sq", "ls6_%d" % mt], accum=lst[:, mt, 6:7])
                TTo("dve", lst[:, mt, 1:2], lst[:, mt, 1:2], lst[:, mt, 6:7], ALU.add, r=["ls1_%d" % mt, "ls6_%d" % mt], w=["ls1_%d" % mt])
                TS("dve", lst[:, mt, 2:3], lst[:, mt, 0:1], 1.0 / E, None, ALU.mult, r=["ls0_%d" % mt], w=["ls2_%d" % mt])
                TTo("dve", lst[:, mt, 3:4], lst[:, mt, 2:3], lst[:, mt, 2:3], ALU.mult, r=["ls2_%d" % mt], w=["ls3_%d" % mt])
                STT(lst[:, mt, 4:5], lst[:, mt, 1:2], 1.0 / E, lst[:, mt, 3:4], ALU.mult, ALU.subtract,
                    r=["ls1_%d" % mt, "ls3_%d" % mt], w=["ls4_%d" % mt])
                ACT(lst[:, mt, 4:5], lst[:, mt, 4:5], AF.Sqrt, bias=LN_EPS, r=["ls4_%d" % mt], w=["ls4_%d" % mt])
                P.emit("dve", lambda e, mt=mt: e.reciprocal(out=lst[:, mt, 5:6], in_=lst[:, mt, 4:5]),
                       reads=["ls4_%d" % mt], writes=["ls5_%d" % mt])
                TS("dve", vh[mt][:], vh[mt][:], lst[:, mt, 2:3], lst[:, mt, 5:6], ALU.subtract, ALU.mult,
                   r=["ls2_%d" % mt, "ls5_%d" % mt, "vh%d" % mt], w=["vh%d" % mt])
            for fc in range(NPAIR):
                s = fc % 2
                g = fc // 2
                DMA("pool", wug[s][:, 0, :, :], gwin_v[:, :, fc * 128:(fc + 1) * 128], "wu%d" % s, w=["wu%d" % s])
                DMA("pool", wug[s][:, 1, :, :], gwin_v[:, :, 2 * E + fc * 128:2 * E + (fc + 1) * 128], "wg%d" % s, w=["wg%d" % s])
                for kc in range(16):
                    MM(psb[2][:, 0:TT1], wug[s][:, 0, kc, :], hnT1[:, kc, :], start=(kc == 0), stop=(kc == 15), r=["wu%d" % s, "hnT"], w=["psb2"])
                for kc in range(16):
                    MM(psb[3][:, 0:TT1], wug[s][:, 1, kc, :], hnT1[:, kc, :], start=(kc == 0), stop=(kc == 15), r=["wg%d" % s, "hnT"], w=["psb3"])
                ACT(tu[s][:], psb[2][:, 0:TT1], AF.Gelu_apprx_tanh, r=["psb2"], w=["tu%d" % s])
                ACT(tg[s][:], psb[3][:, 0:TT1], AF.Silu, r=["psb3"], w=["tg%d" % s])
                TTo("pool", ug[:, fc, :], tu[s][:], tg[s][:], ALU.mult, r=["tu%d" % s, "tg%d" % s], w=["ug%d" % fc])
                for mt in range(NMT):
                    MM(psb[4 + s][:, mt * 128:(mt + 1) * 128], vh[mt][:, fc * 128:(fc + 1) * 128], wsT[:, g, :],
                       r=["vh%d" % mt, "wsT"], w=["psb%d" % (4 + s)])
                STT(tu[s][:].rearrange("p (m i) -> p m i", i=128), psb[4 + s][:, 0:TT1].rearrange("p (m i) -> p m i", i=128),
                    pcol(PB_GLG, fc), biasf[:, fc, None, :].broadcast_to([128, NMT, 128]), ALU.mult, ALU.add,
                    r=["psb%d" % (4 + s), "pp", "biasf", "ug%d" % fc], w=["tu%d" % s])
                TTo("pool", ug[:, fc, :], tu[s][:], ug[:, fc, :], ALU.mult, r=["tu%d" % s, "ug%d" % fc], w=["ug%d" % fc])
            for nb in range(8):
                s = nb % 2
                DMA("pool", wo2[s][:], gwo_v[:, :, nb * 256:(nb + 1) * 256], "wo2_%d" % s, w=["wo2_%d" % s])
                for mt in range(NMT):
                    po = psb[6 + mt % 2][:, 0:256]
                    pk = "psb%d" % (6 + mt % 2)
                    for c in range(NPAIR):
                        MM(po, ug[:, c, mt * 128:(mt + 1) * 128], wo2[s][:, c, :], start=(c == 0), stop=(c == NPAIR - 1),
                           r=["ug%d" % c, "wo2_%d" % s], w=[pk])
                    TTo("dve", h2[mt][:, nb * 256:(nb + 1) * 256], po, h2[mt][:, nb * 256:(nb + 1) * 256], ALU.add,
                        r=[pk, "h2_%d" % mt], w=["h2_%d" % mt])
            for mt in range(NMT):
                rows = slice(t0 + mt * 128, t0 + (mt + 1) * 128)
                hk = "h2_%d" % mt
                MEMSET("pool", lst[:, mt, 6:7], 0.0, w=["fs0_%d" % mt])
                ACT(sq[:], h2[mt][:], AF.Square, r=[hk, "fs0_%d" % mt], w=["n1_sq", "fs0_%d" % mt], accum=lst[:, mt, 6:7])
                ACT(lst[:, mt, 7:8], lst[:, mt, 6:7], AF.Sqrt, bias=RMS_EPS, scale=1.0 / D, r=["fs0_%d" % mt], w=["fs1_%d" % mt])
                P.emit("dve", lambda e, mt=mt: e.reciprocal(out=lst[:, mt, 6:7], in_=lst[:, mt, 7:8]),
                       reads=["fs1_%d" % mt], writes=["fs0_%d" % mt])
                STT(h2[mt][:], h2[mt][:], lst[:, mt, 6:7], growf[:], ALU.mult, ALU.mult, r=[hk, "fs0_%d" % mt, "growf"], w=[hk])
                DMA("sp", y_d[rows, :], h2[mt][:], "st_y%d" % mt, r=[hk], w=["y_all"])
        ar.reset(gmark0, P)

    P.emit("sp", lambda e: e.nop(), reads=["y_all", "h1_all"] + ["oT_d%d" % p_ for p_ in pairs])

    P.finalize()
    from contextlib import ExitStack
    with ExitStack() as es:
        sems = {e: es.enter_context(nc.semaphore("s_" + e)) for e in ENGS}
        dsems = {n: es.enter_context(nc.semaphore("d_" + n)) for n in sorted(P.dma_count)}
        with nc.Block() as block:
            @block.sync
            def _(e):
                P.replay("sp", e, sems, dsems)

            @block.scalar
            def _(e):
                P.replay("act", e, sems, dsems)

            @block.vector
            def _(e):
                P.replay("dve", e, sems, dsems)

            @block.gpsimd
            def _(e):
                P.replay("pool", e, sems, dsems)

            @block.tensor
            def _(e):
                P.replay("pe", e, sems, dsems)
    nc._dbg_out = dbg_out
    nc._prog = P
    nc._arena_peak = ar.peak
    return nc


def make_in_maps(inp):
    pp, rowp, gbs = _host_params(inp)
    cbm, cfm = _consts()
    shared = {
        "pp": pp, "rowp": rowp, "cb": cbm, "cf": cfm,
        "rw_in": np.ascontiguousarray(inp["rwkv_w_in"][0], np.float32),
        "rw_up": np.ascontiguousarray(inp["rwkv_w_up"][0], np.float32),
        "ra_up": np.ascontiguousarray(inp["rwkv_a_up"][0], np.float32),
        "rw_out": np.ascontiguousarray(inp["rwkv_w_out"][0], np.float32),
        "gw_in": np.ascontiguousarray(inp["gmlp_w_in"][0], np.float32),
        "gw_sT": np.ascontiguousarray(np.transpose(inp["gmlp_w_s"][0], (0, 2, 1)), np.float32),
        "gbs": gbs,
        "gw_out": np.ascontiguousarray(inp["gmlp_w_out"][0], np.float32),
    }
    x = np.asarray(inp["x"], np.float32)
    return [dict(shared, x=np.ascontiguousarray(x[b])) for b in range(x.shape[0])]


def kernel(**inputs):
    inp = {k: np.asarray(v) for k, v in inputs.items()}
    nc = build()
    in_maps = make_in_maps(inp)
    res = run_bass_kernel_spmd(nc, in_maps, core_ids=list(range(8)))
    return np.stack([np.asarray(r["y"], np.float32) for r in res.results], axis=0)
```

```python
import numpy as np
import concourse.bass as bass
import concourse.mybir as mybir
from concourse.bass_utils import run_bass_kernel_spmd

F32 = mybir.dt.float32
BF16 = mybir.dt.bfloat16
AF = mybir.ActivationFunctionType
ALU = mybir.AluOpType
AX = mybir.AxisListType

T = 2048
D = 2048
E = 4096
RIN = 16768
SHIFT = 12672
NPAIR = 32
CH = 64
NCHK = T // CH
CDEC = 0.6065306597126334
RMS_EPS = 1e-5
LN_EPS = 1e-5
GN_EPS = 64e-5
ENGS = ["pe", "act", "dve", "pool", "sp"]
MERGE = True


class Prog:
    def __init__(self):
        self.ops = {e: [] for e in ENGS}
        self.res = {}
        self.order = []
        self.dma_count = {}
        self.gdep = None

    def barrier(self):
        self.gdep = None
        self.gdep = self.emit("sp", lambda e: e.nop(), writes=list(self.res.keys()))

    def emit(self, eng, fn, reads=(), writes=(), dma=None):
        idx = len(self.ops[eng])
        deps = set()
        if self.gdep is not None:
            deps.add(self.gdep)
        for r in reads:
            st = self.res.get(r)
            if st is not None and st[0] is not None:
                deps.add(st[0])
        for w in writes:
            st = self.res.get(w)
            if st is not None:
                if st[0] is not None:
                    deps.add(st[0])
                deps.update(st[1])
        if dma is not None:
            c = self.dma_count.get(dma, 0) + 1
            self.dma_count[dma] = c
            tok = ("d", dma, c)
        else:
            tok = ("e", eng, idx)
        deps.discard(tok)
        self.ops[eng].append(dict(fn=fn, deps=deps, dma=dma, tok=tok, inc=False, waits=[]))
        self.order.append((eng, idx))
        for r in reads:
            st = self.res.get(r)
            if st is None:
                self.res[r] = (None, [tok])
            else:
                st[1].append(tok)
        for w in writes:
            self.res[w] = (tok, [])
        return tok

    def finalize(self):
        clock = {e: {} for e in ENGS}
        snap = {}
        for eng, idx in self.order:
            op = self.ops[eng][idx]
            ck = clock[eng]
            waits = []
            for d in sorted(op["deps"], key=lambda t: (t[0], str(t[1]), t[2])):
                key = (d[0], d[1])
                if d[0] == "e" and d[1] == eng and eng == "pe":
                    continue
                if ck.get(key, -1) >= d[2]:
                    continue
                waits.append(d)
                ck[key] = d[2]
                s = snap.get(d)
                if s:
                    for k, v in s.items():
                        if ck.get(k, -1) < v:
                            ck[k] = v
            op["waits"] = waits
            for d in waits:
                if d[0] == "e":
                    self.ops[d[1]][d[2]]["inc"] = True
            snap[op["tok"]] = dict(ck)
        val = {}
        for e in ENGS:
            c = 0
            for i, op in enumerate(self.ops[e]):
                if op["inc"] and op["dma"] is None:
                    c += 1
                    val[("e", e, i)] = c
        self.val = val

    def replay(self, eng, engine_obj, sems, dma_sems):
        for op in self.ops[eng]:
            for d in op["waits"]:
                if d[0] == "e":
                    engine_obj.wait_ge(sems[d[1]], self.val[d])
                else:
                    engine_obj.wait_ge(dma_sems[d[1]], 16 * d[2])
            ins = op["fn"](engine_obj)
            if ins is None:
                continue
            if op["dma"] is not None:
                ins.then_inc(dma_sems[op["dma"]], 16)
            elif op["inc"]:
                ins.then_inc(sems[eng], 1)


class Arena:
    def __init__(self, nc, base=16512, limit=229344):
        self.nc = nc
        self.off = base
        self.limit = limit
        self.n = 0
        self.peak = 0

    def alloc(self, name, shape, dt):
        esz = 4 if dt == F32 else 2
        size = esz
        for s in shape[1:]:
            size *= s
        off = (self.off + 31) // 32 * 32
        assert off + size <= self.limit, (name, off, size)
        self.n += 1
        h = self.nc.alloc_sbuf_tensor_at("%s_%d" % (name, self.n), list(shape), dt, offset=off)
        self.off = off + size
        self.peak = max(self.peak, self.off)
        return h

    def mark(self):
        return self.off

    def reset(self, m, prog=None):
        self.off = m
        if prog is not None:
            prog.barrier()


CB_IDENT, CB_FOLD, CB_BONES, CB_M4, CB_ML, CB_SGN, CB_M01, CB_RESET, CB_N = 0, 128, 192, 320, 1344, 1600, 1856, 1860, 1860 + 2048
CF_NEG, CF_N = 0, 4
(PB_MUP_R, PB_MUP_K, PB_MUP_V, PB_MUN_R, PB_MUN_K, PB_MUN_V, PB_W0_0, PB_W0_1, PB_A0_0, PB_A0_1,
 PB_KK, PB_KA, PB_RK, PB_LNW, PB_LNB, PB_GLG, PB_GLB, PB_C0_R, PB_C0_K, PB_C0_V, PB_OMKA) = range(21)
PL_MUP, PL_MUN, PL_C0 = 21 * 32, 21 * 32 + 4, 21 * 32 + 8
NPP = 21 * 32 + 12


def _consts():
    p = np.arange(128)
    h = p // 64
    i = p % 64
    same = (h[:, None] == h[None, :]).astype(np.float32)
    cb = np.zeros((128, CB_N), np.float32)
    cb[:, CB_IDENT:CB_IDENT + 128] = np.eye(128)
    cb[:, CB_FOLD:CB_FOLD + 64] = (i[:, None] == np.arange(64)[None, :])
    cb[:, CB_BONES:CB_BONES + 128] = same
    for d in range(2):
        if d == 0:
            strict = (i[:, None] < i[None, :])
            incl = (i[:, None] <= i[None, :])
        else:
            strict = (i[:, None] > i[None, :])
            incl = (i[:, None] >= i[None, :])
        o = CB_M4 + d * 512
        cb[:, o:o + 128] = -same * strict
        cb[:, o + 128:o + 256] = -same * incl
        cb[:, o + 256:o + 384] = same * strict
        cb[:, o + 384:o + 512] = same * incl
        cb[:, CB_ML + d * 128:CB_ML + (d + 1) * 128] = -same * strict.T
    cb[:, CB_SGN:CB_SGN + 128] = -1.0
    cb[:, CB_SGN + 128:CB_SGN + 256] = 1.0
    cb[:, CB_M01] = (h == 0)
    cb[:, CB_M01 + 1] = (h == 1)
    cf = np.zeros((128, CF_N), np.float32)
    cb[:, CB_RESET:CB_RESET + 2048] = 1.0
    cb[:, CB_RESET:CB_RESET + 2048:64] = 0.0
    cf[:, CF_NEG] = np.where(h == 0, 0.0, -30000.0)
    cf[:, CF_NEG + 1] = np.where(h == 1, 0.0, -30000.0)
    return cb, cf


def _cols(v):
    v = np.asarray(v, np.float32).reshape(-1, 128)
    return np.ascontiguousarray(v.T)


def _host_params(inp):
    pp = np.zeros((128, NPP), np.float32)
    mp = inp["rwkv_mu_prev"][0]
    mn = inp["rwkv_mu_next"][0]

    def put(blk, v):
        pp[:, blk * 32:(blk + 1) * 32] = _cols(v)

    put(PB_MUP_R, mp[0:E]); put(PB_MUP_K, mp[E:2 * E]); put(PB_MUP_V, mp[2 * E:3 * E])
    put(PB_MUN_R, mn[0:E]); put(PB_MUN_K, mn[E:2 * E]); put(PB_MUN_V, mn[2 * E:3 * E])
    put(PB_W0_0, inp["rwkv_w0"][0, 0]); put(PB_W0_1, inp["rwkv_w0"][0, 1])
    put(PB_A0_0, inp["rwkv_a0"][0, 0]); put(PB_A0_1, inp["rwkv_a0"][0, 1])
    put(PB_KK, inp["rwkv_k_k"][0]); put(PB_KA, inp["rwkv_k_a"][0])
    put(PB_RK, inp["rwkv_r_k"][0].reshape(-1))
    put(PB_LNW, inp["rwkv_ln_w"][0]); put(PB_LNB, inp["rwkv_ln_b"][0])
    put(PB_GLG, inp["gmlp_ln_g"][0]); put(PB_GLB, inp["gmlp_ln_b"][0])
    for j in range(4):
        pp[0:96, PL_MUP + j] = mp[3 * E + 96 * j:3 * E + 96 * (j + 1)]
        pp[0:96, PL_MUN + j] = mn[3 * E + 96 * j:3 * E + 96 * (j + 1)]
    rowp = np.concatenate([
        np.broadcast_to(inp["norm_g"][0][None, :], (128, D)),
        np.broadcast_to(inp["norm_g"][1][None, :], (128, D)),
        np.broadcast_to(inp["final_norm_g"][None, :], (128, D))], axis=1)
    gbs = np.broadcast_to(inp["gmlp_b_s"][0].reshape(1, -1), (128, 16 * 128))
    return pp, np.ascontiguousarray(rowp, np.float32), np.ascontiguousarray(gbs, np.float32)


def build(dbg=None, pairs=None, phases="ASBC", nmt1=2):
    dbg = dbg or {}
    pairs = list(range(NPAIR)) if pairs is None else list(pairs)
    nc = bass.Bass("TRN2", target_bir_lowering=False)
    P = Prog()
    ar = Arena(nc)

    def dram(name, shape, dt, kind):
        return nc.dram_tensor(name, list(shape), dt, kind=kind).ap()

    x_d = dram("x", [T, D], F32, "ExternalInput")
    pp_d = dram("pp", [128, NPP], F32, "ExternalInput")
    rowp_d = dram("rowp", [128, 3 * D], F32, "ExternalInput")
    cb_d = dram("cb", [128, CB_N], F32, "ExternalInput")
    cf_d = dram("cf", [128, CF_N], F32, "ExternalInput")
    rwin_d = dram("rw_in", [D, RIN], F32, "ExternalInput")
    rwup_d = dram("rw_up", [2, 96, E], F32, "ExternalInput")
    raup_d = dram("ra_up", [2, 96, E], F32, "ExternalInput")
    rwout_d = dram("rw_out", [E, D], F32, "ExternalInput")
    gwin_d = dram("gw_in", [D, 3 * E], F32, "ExternalInput")
    gwsT_d = dram("gw_sT", [16, 128, 128], F32, "ExternalInput")
    gbs_d = dram("gbs", [128, 16 * 128], F32, "ExternalInput")
    gwout_d = dram("gw_out", [E, D], F32, "ExternalInput")
    y_d = dram("y", [T, D], F32, "ExternalOutput")
    zs_d = dram("zs_scr", [96, 128, T], F32, "Internal")
    sg_d = dram("sg_scr", [NPAIR, 128, T], BF16, "Internal")
    oT_d = dram("oT_scr", [NPAIR, 128, T], BF16, "Internal")
    h1_d = dram("h1_scr", [T, D], F32, "Internal")
    gwin_b = dram("gwin_bf", [D, 3 * E], BF16, "Internal")
    gwout_b = dram("gwout_bf", [E, D], BF16, "Internal")
    dbg_out = {}

    def MM(out, lhsT, rhs, start=True, stop=True, r=(), w=()):
        P.emit("pe", lambda e: e.matmul(out, lhsT, rhs, start=start, stop=stop), reads=r, writes=w)

    def ACT(out, in_, func, bias=0.0, scale=1.0, r=(), w=(), accum=None):
        if accum is None:
            P.emit("act", lambda e: e.activation(out=out, in_=in_, func=func, bias=bias, scale=scale), reads=r, writes=w)
        else:
            P.emit("act", lambda e: e.activation(out=out, in_=in_, func=func, bias=bias, scale=scale, accum_out=accum), reads=r, writes=w)

    def TTo(eng, out, in0, in1, op, r=(), w=()):
        P.emit(eng, lambda e: e.tensor_tensor(out=out, in0=in0, in1=in1, op=op), reads=r, writes=w)

    def TS(eng, out, in0, s1, s2, op0, op1=None, r=(), w=()):
        if op1 is None:
            P.emit(eng, lambda e: e.tensor_scalar(out=out, in0=in0, scalar1=s1, scalar2=None, op0=op0), reads=r, writes=w)
        else:
            P.emit(eng, lambda e: e.tensor_scalar(out=out, in0=in0, scalar1=s1, scalar2=s2, op0=op0, op1=op1), reads=r, writes=w)

    def STT(out, in0, scalar, in1, op0, op1, r=(), w=()):
        P.emit("dve", lambda e: e.scalar_tensor_tensor(out=out, in0=in0, scalar=scalar, in1=in1, op0=op0, op1=op1), reads=r, writes=w)

    def CP(eng, out, in_, r=(), w=()):
        if eng == "act":
            ACT(out, in_, AF.Copy, r=r, w=w)
        else:
            P.emit(eng, lambda e: e.tensor_copy(out=out, in_=in_), reads=r, writes=w)

    def DMA(eng, out, in_, sem, r=(), w=()):
        P.emit(eng, lambda e: e.dma_start(out=out, in_=in_), reads=r, writes=w, dma=sem)

    def MEMSET(eng, ap, val, w=()):
        P.emit(eng, lambda e: e.memset(ap, val), writes=w)

    def dump(name, ap, shape, r, dt=F32):
        if name not in dbg:
            return
        d = dram("dbg_" + name, shape, dt, "ExternalOutput")
        dbg_out[name] = d
        DMA("sp", d, ap, "dbg_" + name, r=r, w=["dbg_" + name])
        P.emit("sp", lambda e: e.nop(), reads=["dbg_" + name])

    conv_jobs = []
    for i in range(16):
        conv_jobs.append((gwin_b[i * 128:(i + 1) * 128, :], gwin_d[i * 128:(i + 1) * 128, :], "cvA"))
    for i in range(8):
        conv_jobs.append((gwout_b[i * 512:(i + 1) * 512, :], gwout_d[i * 512:(i + 1) * 512, :], "cvB"))
    conv_state = [0]

    def emit_conv(n):
        for _ in range(n):
            if conv_state[0] >= len(conv_jobs):
                return
            o_, i_, sname = conv_jobs[conv_state[0]]
            DMA("pool", o_, i_, "cv%d" % conv_state[0], w=["gwbf%d" % conv_state[0]])
            conv_state[0] += 1

    psb = [nc.alloc_psum_tensor("psb%d" % i, [128, 512], F32) for i in range(8)]

    cb = ar.alloc("cb", [128, CB_N], BF16)
    cf = ar.alloc("cf", [128, CF_N], F32)
    pp = ar.alloc("pp", [128, NPP], F32)
    DMA("pool", cb[:], cb_d, "cst0", w=["cb"])
    DMA("sp", cf[:], cf_d, "cst1", w=["cf"])
    DMA("sp", pp[:], pp_d, "cst2", w=["pp"])
    ident = cb[:, CB_IDENT:CB_IDENT + 128]
    fold = cb[:, CB_FOLD:CB_FOLD + 64]
    bones = cb[:, CB_BONES:CB_BONES + 128]
    resetm = cb[:, CB_RESET:CB_RESET + 2048]

    def pcol(blk, c):
        return pp[:, blk * 32 + c:blk * 32 + c + 1]

    for j in range(3):
        a = pp[:, (PB_MUP_R + j) * 32:(PB_MUP_R + j + 1) * 32]
        b = pp[:, (PB_MUN_R + j) * 32:(PB_MUN_R + j + 1) * 32]
        c = pp[:, (PB_C0_R + j) * 32:(PB_C0_R + j + 1) * 32]
        TTo("dve", c, a, b, ALU.add, r=["pp"], w=["pp"])
        TS("dve", c, c, -1.0, 1.0, ALU.mult, ALU.add, r=["pp"], w=["pp"])
    TTo("dve", pp[:, PL_C0:PL_C0 + 4], pp[:, PL_MUP:PL_MUP + 4], pp[:, PL_MUN:PL_MUN + 4], ALU.add, r=["pp"], w=["pp"])
    TS("dve", pp[:, PL_C0:PL_C0 + 4], pp[:, PL_C0:PL_C0 + 4], -1.0, 1.0, ALU.mult, ALU.add, r=["pp"], w=["pp"])
    TS("dve", pp[:, PB_OMKA * 32:(PB_OMKA + 1) * 32], pp[:, PB_KA * 32:(PB_KA + 1) * 32], -1.0, 1.0, ALU.mult, ALU.add, r=["pp"], w=["pp"])

    gmark0 = ar.mark()
    tdw = ar.alloc("tdw", [128, 4, T], BF16)
    gmark = ar.mark()

    def norm_transpose(src_rows, ntile, grow, hnT, col0, bufs, tag, pre=None):
        xt0, sq, hn, st = bufs
        for mt in range(ntile):
            s = mt % 2
            if pre is None:
                xt = xt0
                xkey = "%s_xt%d" % (tag, s)
                DMA("sp", xt[s][:], src_rows(mt), "%s_x%d" % (tag, s), w=[xkey])
            else:
                xt = {s: pre[mt][0]}
                xkey = pre[mt][1]
            MEMSET("pool", st[:, 0:1], 0.0, w=[tag + "_ss"])
            ACT(sq[:], xt[s][:], AF.Square, r=[xkey, tag + "_ss"], w=[tag + "_sq", tag + "_ss"], accum=st[:, 0:1])
            ACT(st[:, 1:2], st[:, 0:1], AF.Sqrt, bias=RMS_EPS, scale=1.0 / D, r=[tag + "_ss"], w=[tag + "_sd"])
            P.emit("dve", lambda e: e.reciprocal(out=st[:, 2:3], in_=st[:, 1:2]), reads=[tag + "_sd"], writes=[tag + "_rs"])
            STT(hn[s][:], xt[s][:], st[:, 2:3], grow, ALU.mult, ALU.mult,
                r=[xkey, tag + "_rs", "grow"], w=["%s_hn%d" % (tag, s)])
            for q in range(4):
                pb = psb[q % 2]
                for j in range(4):
                    kc = q * 4 + j
                    MM(pb[:, j * 128:(j + 1) * 128], hn[s][:, kc * 128:(kc + 1) * 128], ident,
                       r=["%s_hn%d" % (tag, s), "cb"], w=["psb%d" % (q % 2)])
                dst = hnT[:, q * 4:(q + 1) * 4, col0 + mt * 128:col0 + (mt + 1) * 128]
                src = pb[:, :].rearrange("p (j t) -> p j t", j=4)
                if q % 2 == 0:
                    CP("act", dst, src, r=["psb%d" % (q % 2)], w=["hnT"])
                else:
                    CP("dve", dst, src, r=["psb%d" % (q % 2)], w=["hnT"])

    if "A" in phases:
        hnT = ar.alloc("hnT", [128, 16, T + 2], BF16)
        grow = ar.alloc("grow", [128, D], F32)
        DMA("sp", grow[:], rowp_d[:, 0:D], "grow", w=["grow"])
        MEMSET("pool", hnT[:, :, 0:1], 0.0, w=["hnT"])
        MEMSET("pool", hnT[:, :, T + 1:T + 2], 0.0, w=["hnT"])
        m0 = ar.mark()
        xt = [ar.alloc("xt", [128, D], F32) for _ in range(2)]
        sq = ar.alloc("sq", [128, D], BF16)
        hn = [ar.alloc("hn", [128, D], BF16) for _ in range(2)]
        st = ar.alloc("st", [128, 4], F32)
        norm_transpose(lambda mt: x_d[mt * 128:(mt + 1) * 128, :], 16, grow[:], hnT, 1, (xt, sq, hn, st), "n0")
        ar.reset(m0, P)
        dump("hnT", hnT[:, 0, :], [128, T + 2], ["hnT"], BF16)

        NW = 3
        wt = [ar.alloc("wt", [128, 16, 128], BF16) for _ in range(NW)]
        zraw = [ar.alloc("zraw", [128, T + 2], F32) for _ in range(2)]
        zsb = [ar.alloc("zsb", [128, T], F32) for _ in range(2)]
        sgb = [ar.alloc("sgb", [128, T], BF16) for _ in range(2)]
        rwin_v = rwin_d.rearrange("(kc kk) n -> kk kc n", kk=128)
        chunks = []
        need = set()
        for p_ in pairs:
            need.update([p_, 32 + p_, 64 + p_])
        for j in range(4):
            chunks.append(("lora", 3 * E + 96 * j, 96, j))
        for cc in sorted(need):
            chunks.append(("rkv", cc * 128, 128, cc))
        for p_ in pairs:
            chunks.append(("gate", SHIFT + p_ * 128, 128, p_))

        def load_w(i):
            kind, c0_, M, info = chunks[i]
            s = i % NW
            DMA("pool", wt[s][:, :, 0:M], rwin_v[:, :, c0_:c0_ + M], "wtA%d" % s, w=["wtA%d" % s])

        for i in range(min(2, len(chunks))):
            load_w(i)
        for i, (kind, c0_, M, info) in enumerate(chunks):
            s = i % NW
            zr = zraw[i % 2]
            zk = "zraw%d" % (i % 2)
            if i + 2 < len(chunks):
                load_w(i + 2)
            for blk in range(5):
                pb = psb[blk % 2]
                for kc in range(16):
                    MM(pb[0:M, 0:410], wt[s][:, kc, 0:M], hnT[:, kc, blk * 410:(blk + 1) * 410],
                       start=(kc == 0), stop=(kc == 15), r=["wtA%d" % s, "hnT"], w=["psb%d" % (blk % 2)])
                CP("act" if blk % 2 == 0 else "dve", zr[0:M, blk * 410:(blk + 1) * 410], pb[0:M, 0:410],
                   r=["psb%d" % (blk % 2)], w=["%s_%d" % (zk, blk)])
            zr_all = ["%s_%d" % (zk, b) for b in range(5)]
            if kind == "gate":
                so = sgb[i % 2]
                ACT(so[:], zr[:, 1:T + 1], AF.Silu, r=zr_all, w=["sgb%d" % (i % 2)])
                DMA("sp", sg_d[info], so[:], "sgst%d" % (i % 2), r=["sgb%d" % (i % 2)], w=["sg_d%d" % info])
                continue
            if kind == "rkv":
                q, c = info // 32, info % 32
                c0c, mpc, mnc = pcol(PB_C0_R + q, c), pcol(PB_MUP_R + q, c), pcol(PB_MUN_R + q, c)
            else:
                c0c, mpc, mnc = pp[0:96, PL_C0 + info:PL_C0 + info + 1], pp[0:96, PL_MUP + info:PL_MUP + info + 1], pp[0:96, PL_MUN + info:PL_MUN + info + 1]
            zo = zsb[i % 2]
            zok = "zsb%d" % (i % 2)
            ACT(zo[0:M, :], zr[0:M, 1:T + 1], AF.Copy, scale=c0c, r=zr_all + ["pp"], w=[zok])
            STT(zo[0:M, :], zr[0:M, 0:T], mpc, zo[0:M, :], ALU.mult, ALU.add, r=zr_all + ["pp", zok], w=[zok])
            STT(zo[0:M, :], zr[0:M, 2:T + 2], mnc, zo[0:M, :], ALU.mult, ALU.add, r=zr_all + ["pp", zok], w=[zok])
            if kind == "rkv":
                DMA("sp", zs_d[info], zo[:], "zsst%d" % (i % 2), r=[zok], w=["zs_d%d" % info])
            else:
                if info < 2:
                    ACT(tdw[0:96, info, :], zo[0:96, :], AF.Tanh, r=[zok], w=["tdw%d" % info])
                else:
                    CP("act", tdw[0:96, info, :], zo[0:96, :], r=[zok], w=["tdw%d" % info])
        ar.reset(gmark, P)

    if "S" in phases:
        def v3(ap):
            return ap.rearrange("p (c t) -> p c t", t=CH)

        def bc4(ap):
            return v3(ap)[:, :, None, :].broadcast_to([128, NCHK, 2, CH])

        def flat(t5):
            return t5[:].rearrange("p c q h t -> p (c q h t)")

        rwup_v = [rwup_d[d] for d in range(2)]
        raup_v = [raup_d[d] for d in range(2)]
        r_f = ar.alloc("r_f", [128, T], F32)
        k_f = ar.alloc("k_f", [128, T], F32)
        v_f = ar.alloc("v_f", [128, T], F32)
        sg = ar.alloc("sg", [128, T], BF16)
        kk = ar.alloc("kk", [128, T], F32)
        ksum = ar.alloc("ksum", [128, T], BF16)
        Yb = ar.alloc("Yb", [128, NCHK, CH], F32)
        Vst = ar.alloc("Vst", [128, NCHK, CH], BF16)
        wup = ar.alloc("wup", [128, 2, 2, 128], BF16)
        t_sw = ar.alloc("t_sw", [128, T], F32)
        t_P = ar.alloc("t_P", [128, T], F32)
        t_a = ar.alloc("t_a", [128, T], F32)
        t_kd = ar.alloc("t_kd", [128, T], F32)
        t_b = ar.alloc("t_b", [128, T], F32)
        ex0_ = ar.alloc("ex", [128, NCHK, 2, CH], F32)
        ex = [ex0_, ex0_]
        ar_bd = ar.alloc("ar_bd", [128, NCHK, 2, 2, CH], BF16)
        bk_bd = ar.alloc("bk_bd", [128, NCHK, 2, 2, CH], BF16)
        bkw_bd = ar.alloc("bkw_bd", [128, NCHK, 2, 2, CH], BF16)
        tot = ar.alloc("tot", [128, NCHK], F32)
        WC = ar.alloc("WC", [128, NCHK], F32)
        gst = ar.alloc("gst", [128, 6, NCHK], F32)
        S_b = ar.alloc("S_b", [128, CH], BF16)
        S_f2 = ar.alloc("S_f2", [128, CH], F32)
        GB = 4
        Amat4 = [ar.alloc("Amat4", [128, GB, 512], BF16) for _ in range(2)]
        Lm4 = ar.alloc("Lm4", [128, GB, 128], BF16)
        LGL = [ar.alloc("LGL", [128, GB, 128], BF16) for _ in range(2)]
        LGG = [ar.alloc("LGG", [128, GB, 128], BF16) for _ in range(2)]
        T4 = [ar.alloc("T4", [128, GB, 128], BF16) for _ in range(2)]
        BK4 = [ar.alloc("BK4", [128, GB, 256], BF16) for _ in range(2)]
        Xb = ar.alloc("Xb", [128, CH], BF16)
        Ub = ar.alloc("Ub", [128, CH], BF16)
        ynx = flat(ar_bd)[:, 0:2 * T].rearrange("p (c h t) -> p c h t", h=2, t=CH)
        negm = [cf[:, CF_NEG + h:CF_NEG + h + 1] for h in range(2)]
        m01b = cb[:, CB_M01:CB_M01 + 2]


        for p_ in pairs:
            emit_conv(1)
            DMA("sp", r_f[:], zs_d[p_], "ld_r", w=["r_f"])
            DMA("sp", k_f[:], zs_d[32 + p_], "ld_k", w=["k_f"])
            DMA("sp", v_f[:], zs_d[64 + p_], "ld_v", w=["v_f"])
            DMA("sp", sg[:], sg_d[p_], "ld_sg", w=["sg"])
            for d in range(2):
                DMA("pool", wup[0:96, 0, d, :], rwup_v[d][:, p_ * 128:(p_ + 1) * 128], "ld_wu%d" % d, w=["wup0%d" % d])
                DMA("pool", wup[0:96, 1, d, :], raup_v[d][:, p_ * 128:(p_ + 1) * 128], "ld_au%d" % d, w=["wup1%d" % d])
            sqt = flat(ar_bd)[:, 0:T]
            ACT(sqt, k_f[:], AF.Square, scale=pcol(PB_KK, p_), r=["k_f", "pp"], w=["ar_bd"])
            for blk in range(4):
                bs = slice(blk * 512, (blk + 1) * 512)
                MM(psb[2][:, :], bones, sqt[:, bs], r=["ar_bd", "cb"], w=["psb2"])
                ACT(t_P[:, bs], psb[2][:, :], AF.Sqrt, bias=1e-24, r=["psb2"], w=["t_P"])
            P.emit("dve", lambda e: e.reciprocal(out=t_P[:], in_=t_P[:]), reads=["t_P"], writes=["t_P"])
            STT(kk[:], k_f[:], pcol(PB_KK, p_), t_P[:], ALU.mult, ALU.mult, r=["k_f", "pp", "t_P"], w=["kk"])
            dump("kk", kk[:], [128, T], ["kk"])
            vbd = flat(bkw_bd)[:, 0:2 * T].rearrange("p (c h t) -> p c h t", h=2, t=CH)
            TTo("pool", vbd, bc4(v_f[:]), m01b[:, None, :, None].broadcast_to([128, NCHK, 2, CH]),
                ALU.mult, r=["v_f", "cb"], w=["bkw_bd"])
            for c8 in range(4):
                for j in range(8):
                    c = c8 * 8 + j
                    MM(psb[3][:, j * 64:(j + 1) * 64], vbd[:, c, :, :].rearrange("p h t -> p (h t)"), fold, r=["bkw_bd", "cb"], w=["psb3"])
                CP("act", Vst[:, c8 * 8:(c8 + 1) * 8, :], psb[3][:, :].rearrange("p (c v) -> p c v", v=CH), r=["psb3"], w=["Vst"])

            for d in range(2):
                sgn = -CDEC if d == 0 else CDEC
                for blk in range(4):
                    bs = slice(blk * 512, (blk + 1) * 512)
                    MM(psb[2][:, :], wup[0:96, 0, d, :], tdw[0:96, d, bs], r=["wup0%d" % d, "tdw%d" % d], w=["psb2"])
                    ACT(t_sw[:, bs], psb[2][:, :], AF.Sigmoid, bias=pcol(PB_W0_0 + d, p_), r=["psb2", "pp"], w=["t_sw"])
                    MM(psb[3][:, :], wup[0:96, 1, d, :], tdw[0:96, 2 + d, bs], r=["wup1%d" % d, "tdw%d" % (2 + d)], w=["psb3"])
                    ACT(t_a[:, bs], psb[3][:, :], AF.Sigmoid, bias=pcol(PB_A0_0 + d, p_), r=["psb3", "pp"], w=["t_a"])
                TS("pool", t_kd[:], t_a[:], pcol(PB_KA, p_), pcol(PB_OMKA, p_), ALU.mult, ALU.add, r=["t_a", "pp"], w=["t_kd"])
                TTo("pool", t_kd[:], t_kd[:], k_f[:], ALU.mult, r=["t_kd", "k_f"], w=["t_kd"])
                if d == 0:
                    CP("pool", ksum[:], t_kd[:], r=["t_kd"], w=["ksum"])
                else:
                    TTo("pool", ksum[:], ksum[:], t_kd[:], ALU.add, r=["ksum", "t_kd"], w=["ksum"])
                TTo("dve", t_b[:], kk[:], t_a[:], ALU.mult, r=["kk", "t_a"], w=["t_b"])
                P.emit("dve", lambda e: e.tensor_tensor_scan(out=t_P[:], data0=resetm, data1=t_sw[:], initial=0.0,
                                                             op0=ALU.mult, op1=ALU.add),
                       reads=["cb", "t_sw"], writes=["t_P"])
                CP("dve", tot[:, :, None], v3(t_P[:])[:, :, CH - 1:CH], r=["t_P"], w=["tot"])
                ACT(WC[:], tot[:], AF.Exp, scale=-CDEC, r=["tot"], w=["WC"])
                TTo("pool", t_sw[:], t_P[:], t_sw[:], ALU.subtract, r=["t_P", "t_sw"], w=["t_sw"])
                totb = tot[:, :, None].broadcast_to([128, NCHK, CH])
                if d == 0:
                    Lr, Lx = t_P, t_sw
                else:
                    TTo("pool", v3(t_sw[:]), v3(t_sw[:]), totb, ALU.subtract, r=["t_sw", "tot"], w=["t_sw"])
                    TTo("dve", v3(t_P[:]), v3(t_P[:]), totb, ALU.subtract, r=["t_P", "tot"], w=["t_P"])
                    Lr, Lx = t_sw, t_P
                lrk = "t_P" if Lr is t_P else "t_sw"
                lxk = "t_P" if Lx is t_P else "t_sw"
                for h in range(2):
                    ACT(ex[0][:, :, h, :], v3(Lr[:]), AF.Exp, bias=negm[h], scale=sgn, r=[lrk, "cf"], w=["ex0_%d" % h])
                TTo("dve", ar_bd[:, :, 1, :, :], bc4(r_f[:]), ex[0][:], ALU.mult,
                    r=["r_f", "ex0_0", "ex0_1"], w=["ar_bd"])
                for h in range(2):
                    ACT(ex[1][:, :, h, :], v3(Lx[:]), AF.Exp, bias=negm[h], scale=sgn, r=[lxk, "cf"], w=["ex0_%d" % h])
                TTo("pool", ar_bd[:, :, 0, :, :], bc4(kk[:]), ex[1][:], ALU.mult,
                    r=["kk", "ex0_0", "ex0_1"], w=["ar_bd"])
                for h in range(2):
                    ACT(ex[0][:, :, h, :], v3(Lr[:]), AF.Exp, bias=negm[h], scale=-sgn, r=[lrk, "cf"], w=["ex0_%d" % h])
                TTo("dve", bk_bd[:, :, 0, :, :], bc4(t_b[:]), ex[0][:], ALU.mult,
                    r=["t_b", "ex0_0", "ex0_1"], w=["bk_bd"])
                TTo("pool", bk_bd[:, :, 1, :, :], bc4(t_kd[:]), ex[0][:], ALU.mult,
                    r=["t_kd", "ex0_0", "ex0_1"], w=["bk_bd"])
                wcb = WC[:, :, None, None].broadcast_to([128, NCHK, 2, CH])
                for q in range(2):
                    TTo("dve" if q == 0 else "pool", bkw_bd[:, :, q, :, :], bk_bd[:, :, q, :, :], wcb, ALU.mult,
                        r=["bk_bd", "WC"], w=["bkw_bd"])
                if p_ == pairs[0]:
                    dump("rt%d" % d, ar_bd[:, :, 1, :, :], [128, NCHK, 2, CH], ["ar_bd"], BF16)
                    dump("at%d" % d, ar_bd[:, :, 0, :, :], [128, NCHK, 2, CH], ["ar_bd"], BF16)
                MEMSET("pool", S_b[:], 0.0, w=["S_b"])
                MEMSET("dve", S_f2[:], 0.0, w=["S_f2"])
                m4 = cb[:, CB_M4 + d * 512:CB_M4 + (d + 1) * 512]
                mL = cb[:, CB_ML + d * 128:CB_ML + (d + 1) * 128]
                sgn2 = cb[:, None, CB_SGN:CB_SGN + 256].broadcast_to([128, 2, 256])
                corder = list(range(NCHK)) if d == 0 else list(range(NCHK - 1, -1, -1))
                NG = NCHK // GB

                def chv(t5, c, q):
                    return t5[:, c, q, :, :].rearrange("p h t -> p (h t)")

                def pre_ops(g):
                    gp = g % 2
                    cl = corder[g * GB:(g + 1) * GB]
                    A4, Tg, BKg = Amat4[gp], T4[gp], BK4[gp]
                    kA, kT, kBK = "Amat4_%d" % gp, "T4_%d" % gp, "BK4_%d" % gp
                    ops = []
                    for j, c in enumerate(cl):
                        bk = 4 + j % 2
                        ar_c = ar_bd[:, c, :, :, :].rearrange("p q h t -> p (q h t)")
                        ops.append(lambda c=c, bk=bk, ar_c=ar_c: MM(psb[bk][:, 0:256], chv(bk_bd, c, 0), ar_c, r=["bk_bd", "ar_bd"], w=["psb%d" % bk]))
                        ops.append(lambda c=c, bk=bk, ar_c=ar_c: MM(psb[bk][:, 256:512], chv(bk_bd, c, 1), ar_c, r=["bk_bd", "ar_bd"], w=["psb%d" % bk]))
                        ops.append(lambda j=j, bk=bk: TTo("dve", A4[:, j, :], psb[bk][:, :], m4, ALU.mult, r=["psb%d" % bk, "cb"], w=[kA + "_%d" % j]))
                    for j, c in enumerate(cl):
                        ops.append(lambda j=j, c=c: MM(psb[6][:, j * 128:(j + 1) * 128], chv(ar_bd, c, 0), chv(bk_bd, c, 0), r=["ar_bd", "bk_bd"], w=["psb6"]))
                    ops.append(lambda: TTo("dve", Lm4[:], psb[6][:, :].rearrange("p (j t) -> p j t", j=GB),
                                           mL[:, None, :].broadcast_to([128, GB, 128]), ALU.mult, r=["psb6", "cb"], w=["Lm4"]))
                    akeys = [kA + "_%d" % j for j in range(GB)]
                    ops.append(lambda: TTo("pool", Tg[:], A4[:, :, 0:128], ident[:, None, :].broadcast_to([128, GB, 128]), ALU.add,
                                           r=akeys + ["cb"], w=[kT]))
                    for it in range(5):
                        if it == 0:
                            Lc, Gc, lck = Lm4, A4, ["Lm4"] + akeys
                        else:
                            Lc, Gc, lck = LGL[(it - 1) % 2], LGG[(it - 1) % 2], ["LGL%d" % ((it - 1) % 2), "LGG%d" % ((it - 1) % 2)]
                        Ln, Gn = LGL[it % 2], LGG[it % 2]
                        for j in range(GB):
                            ops.append(lambda j=j, Lc=Lc, Gc=Gc, lck=lck: MM(psb[5][:, j * 128:(j + 1) * 128], Gc[:, j, 0:128], Lc[:, j, 0:128], r=lck, w=["psb5"]))
                        ops.append(lambda Ln=Ln, it=it: CP("act", Ln[:], psb[5][:, :].rearrange("p (j t) -> p j t", j=GB), r=["psb5"], w=["LGL%d" % (it % 2)]))
                        if it < 4:
                            for j in range(GB):
                                ops.append(lambda j=j, Lc=Lc, Gc=Gc, lck=lck: MM(psb[2][:, j * 128:(j + 1) * 128], Lc[:, j, 0:128], Gc[:, j, 0:128], r=lck, w=["psb2"]))
                            ops.append(lambda Gn=Gn, it=it: CP("act", Gn[:], psb[2][:, :].rearrange("p (j t) -> p j t", j=GB), r=["psb2"], w=["LGG%d" % (it % 2)]))
                        for j in range(GB):
                            ops.append(lambda j=j, Ln=Ln, it=it: MM(psb[3][:, j * 128:(j + 1) * 128], Ln[:, j, :], Tg[:, j, :], r=["LGL%d" % (it % 2), kT], w=["psb3"]))
                        ops.append(lambda: TTo("dve", Tg[:], Tg[:], psb[3][:, :].rearrange("p (j t) -> p j t", j=GB), ALU.add, r=[kT, "psb3"], w=[kT]))
                    for half in range(2):
                        bk = 0
                        for jj in range(2):
                            c = cl[half * 2 + jj]
                            ops.append(lambda c=c, jj=jj, bk=bk: MM(psb[bk][:, jj * 256:jj * 256 + 128], chv(bkw_bd, c, 0), ident, r=["bkw_bd", "cb"], w=["psb%d" % bk]))
                            ops.append(lambda c=c, jj=jj, bk=bk: MM(psb[bk][:, jj * 256 + 128:jj * 256 + 256], chv(bkw_bd, c, 1), ident, r=["bkw_bd", "cb"], w=["psb%d" % bk]))
                        ops.append(lambda half=half, bk=bk: TTo("dve", BKg[:, half * 2:half * 2 + 2, :], psb[bk][:, :].rearrange("p (j t) -> p j t", j=2),
                                                                sgn2, ALU.mult, r=["psb%d" % bk, "cb"], w=[kBK + "_%d" % half]))
                    return ops

                def chain_ops(g):
                    gp = g % 2
                    cl = corder[g * GB:(g + 1) * GB]
                    A4, Tg, BKg = Amat4[gp], T4[gp], BK4[gp]
                    kT = "T4_%d" % gp
                    ops = []
                    for j, c in enumerate(cl):
                        kAj = "Amat4_%d_%d" % (gp, j)
                        kBKj = "BK4_%d_%d" % (gp, j // 2)
                        ops.append(lambda c=c: MM(psb[7][:, 0:64], chv(ar_bd, c, 0), S_b[:], start=True, stop=False, r=["ar_bd", "S_b"], w=["psb7"]))
                        ops.append(lambda c=c, j=j, kAj=kAj: MM(psb[7][:, 0:64], A4[:, j, 256:384], Vst[:, c, :], start=False, stop=True, r=[kAj, "Vst"], w=["psb7"]))
                        ops.append(lambda: CP("act", Xb[:], psb[7][:, 0:64], r=["psb7"], w=["Xb"]))
                        ops.append(lambda j=j: MM(psb[7][:, 64:128], Tg[:, j, :], Xb[:], r=[kT, "Xb"], w=["psb7"]))
                        ops.append(lambda: CP("act", Ub[:], psb[7][:, 64:128], r=["psb7"], w=["Ub"]))
                        import os as _os2
                        _v = "a"
                        sgrp = [
                            lambda j=j, kBKj=kBKj: MM(psb[7][:, 192:256], BKg[:, j, 0:128], Ub[:], start=True, stop=False, r=[kBKj, "Ub"], w=["psb7"]),
                            lambda j=j, c=c, kBKj=kBKj: MM(psb[7][:, 192:256], BKg[:, j, 128:256], Vst[:, c, :], start=False, stop=True, r=[kBKj, "Vst"], w=["psb7"]),
                        ]
                        ygrp = [
                            lambda c=c: MM(psb[1][:, 0:64], chv(ar_bd, c, 1), S_b[:], start=True, stop=False, r=["ar_bd", "S_b"], w=["psb1"]),
                            lambda j=j, kAj=kAj: MM(psb[1][:, 0:64], A4[:, j, 128:256], Ub[:], start=False, stop=False, r=[kAj, "Ub"], w=["psb1"]),
                            lambda j=j, c=c, kAj=kAj: MM(psb[1][:, 0:64], A4[:, j, 384:512], Vst[:, c, :], start=False, stop=True, r=[kAj, "Vst"], w=["psb1"]),
                        ]
                        if "b" in _v:
                            supd = [lambda c=c: STT(S_f2[:], S_f2[:], WC[:, c:c + 1], psb[7][:, 192:256], ALU.mult, ALU.add, r=["S_f2", "WC", "psb7"], w=["S_f2"]),
                                    lambda: CP("act", S_b[:], S_f2[:], r=["S_f2"], w=["S_b"])]
                        else:
                            supd = [lambda c=c: STT(S_b[:], S_b[:], WC[:, c:c + 1], psb[7][:, 192:256], ALU.mult, ALU.add, r=["S_b", "WC", "psb7"], w=["S_b"])]
                        if d == 0:
                            yev = [lambda c=c: CP("act", Yb[:, c, :], psb[1][:, 0:64], r=["psb1"], w=["Yb%d" % c])]
                        else:
                            yev = [lambda c=c: TTo("dve", Yb[:, c, :], Yb[:, c, :], psb[1][:, 0:64], ALU.add, r=["psb1", "Yb%d" % c], w=["Yb%d" % c])]
                        if "a" in _v:
                            ops.extend(ygrp + sgrp + supd + yev)
                        else:
                            ops.extend(sgrp + ygrp + supd + yev)
                    return ops

                def merged(a, b):
                    if not MERGE:
                        for f in a:
                            f()
                        for f in b:
                            f()
                        return
                    na, nb = len(a), len(b)
                    ia = ib = 0
                    while ia < na or ib < nb:
                        if ib >= nb or (ia < na and ia * nb <= ib * na):
                            a[ia](); ia += 1
                        else:
                            b[ib](); ib += 1

                import os as _os
                _sk = ""
                for op_ in ([] if "pre" in _sk else pre_ops(0)):
                    op_()
                for g in range(NG):
                    merged([] if "chain" in _sk else chain_ops(g), pre_ops(g + 1) if (g + 1 < NG and "pre" not in _sk) else [])
            ykeys = ["Yb%d" % c for c in range(NCHK)]
            if p_ == pairs[0]:
                dump("Y", Yb[:], [128, NCHK, CH], ykeys)
            s1, s2_, mean, msq, var, rstd = [gst[:, j, :] for j in range(6)]
            P.emit("dve", lambda e: e.tensor_reduce(out=s1, in_=Yb[:], axis=AX.X, op=ALU.add), reads=ykeys, writes=["gst0"])
            sqf = ex[0][:].rearrange("p c h t -> p (c h t)")[:, 0:T]
            TTo("pool", sqf, Yb[:].rearrange("p c v -> p (c v)"), Yb[:].rearrange("p c v -> p (c v)"), ALU.mult,
                r=ykeys + ["ex0_0", "ex0_1"], w=["ex0_0", "ex0_1"])
            P.emit("dve", lambda e: e.tensor_reduce(out=s2_, in_=v3(sqf), axis=AX.X, op=ALU.add), reads=["ex0_0", "ex0_1"], writes=["gst1"])
            TS("dve", mean, s1, 1.0 / CH, None, ALU.mult, r=["gst0"], w=["gst2"])
            TTo("dve", msq, mean, mean, ALU.mult, r=["gst2"], w=["gst3"])
            STT(var, s2_, 1.0 / CH, msq, ALU.mult, ALU.subtract, r=["gst1", "gst3"], w=["gst4"])
            ACT(var, var, AF.Sqrt, bias=GN_EPS, r=["gst4"], w=["gst4"])
            P.emit("dve", lambda e: e.reciprocal(out=rstd, in_=var), reads=["gst4"], writes=["gst5"])
            TTo("dve", Yb[:], Yb[:], mean[:, :, None].broadcast_to([128, NCHK, CH]), ALU.subtract, r=ykeys + ["gst2"], w=ykeys)
            MEMSET("pool", ynx, 0.0, w=["ar_bd"])
            for h in range(2):
                hs = slice(h * 64, (h + 1) * 64)
                TTo("dve" if h == 0 else "pool", ynx[hs, :, h, :], Yb[hs, :, :], rstd[hs, :, None].broadcast_to([64, NCHK, CH]), ALU.mult,
                    r=ykeys + ["gst5"], w=["ar_bd"])
            o1 = t_a
            for c8 in range(4):
                for j in range(8):
                    c = c8 * 8 + j
                    MM(psb[3][:, j * 64:(j + 1) * 64], ynx[:, c, :, :].rearrange("p h t -> p (h t)"), fold, r=["ar_bd", "cb"], w=["psb3"])
                ACT(o1[:, c8 * 512:(c8 + 1) * 512], psb[3][:, :], AF.Identity, bias=pcol(PB_LNB, p_), scale=pcol(PB_LNW, p_),
                    r=["psb3", "pp"], w=["t_a"])
            TTo("pool", ksum[:], ksum[:], r_f[:], ALU.mult, r=["ksum", "r_f"], w=["ksum"])
            q2 = flat(bk_bd)[:, 0:T]
            TS("dve", q2, ksum[:], pcol(PB_RK, p_), None, ALU.mult, r=["ksum", "pp"], w=["bk_bd"])
            for blk in range(4):
                bs = slice(blk * 512, (blk + 1) * 512)
                MM(psb[2][:, :], bones, q2[:, bs], r=["bk_bd", "cb"], w=["psb2"])
                TTo("dve", t_b[:, bs], psb[2][:, :], v_f[:, bs], ALU.mult, r=["psb2", "v_f"], w=["t_b"])
            TTo("pool", o1[:], o1[:], t_b[:], ALU.add, r=["t_a", "t_b"], w=["t_a"])
            oTb = flat(bkw_bd)[:, 2 * T:3 * T]
            TTo("pool", oTb, o1[:], sg[:], ALU.mult, r=["t_a", "sg"], w=["bkw_bd"])
            DMA("sp", oT_d[p_], oTb, "st_oT", r=["bkw_bd"], w=["oT_d%d" % p_])
            if p_ == pairs[0]:
                dump("oT", oTb, [128, T], ["bkw_bd"], BF16)
        ar.reset(gmark, P)

    def out_proj(src_oT, wout_dram, res_rows, final, tag, grow=None):
        m0 = ar.mark()
        oTt = ar.alloc("oTt", [128, NPAIR, 512], BF16)
        wo = [ar.alloc("wo", [128, NPAIR, 512], BF16) for _ in range(2)]
        xr = [ar.alloc("xr", [128, D], F32) for _ in range(4)]
        sq = ar.alloc("sqo", [128, D], BF16)
        st = ar.alloc("sto", [128, 4, 4], F32)
        wo_v = wout_dram.rearrange("(c p) n -> p c n", p=128)
        oT_v = src_oT.rearrange("c p t -> p c t")
        for tb in range(4):
            DMA("sp", oTt[:], oT_v[:, :, tb * 512:(tb + 1) * 512], tag + "_oT", r=["oT_all"], w=[tag + "oTt"])
            for mt in range(4):
                DMA("sp", xr[mt][:], res_rows(tb * 4 + mt), "%s_xr%d" % (tag, mt), r=["res_all"], w=["%sxr%d" % (tag, mt)])
            for nb in range(4):
                s = nb % 2
                DMA("pool", wo[s][:], wo_v[:, :, nb * 512:(nb + 1) * 512], "%s_wo%d" % (tag, s), w=["%swo%d" % (tag, s)])
                for mt in range(4):
                    po = psb[mt][:, :]
                    pk = "psb%d" % mt
                    for c in range(NPAIR):
                        MM(po, oTt[:, c, mt * 128:(mt + 1) * 128], wo[s][:, c, :], start=(c == 0), stop=(c == NPAIR - 1),
                           r=[tag + "oTt", "%swo%d" % (tag, s)], w=[pk])
                    TTo("dve", xr[mt][:, nb * 512:(nb + 1) * 512], po, xr[mt][:, nb * 512:(nb + 1) * 512], ALU.add,
                        r=[pk, "%sxr%d" % (tag, mt)], w=["%sxr%d" % (tag, mt)])
            for mt in range(4):
                rows = slice((tb * 4 + mt) * 128, (tb * 4 + mt + 1) * 128)
                xk = "%sxr%d" % (tag, mt)
                if not final:
                    DMA("sp", h1_d[rows, :], xr[mt][:], "%s_st%d" % (tag, mt), r=[xk], w=["h1_all"])
                else:
                    ACT(sq[:], xr[mt][:], AF.Square, r=[xk], w=[tag + "sq", tag + "ss%d" % mt], accum=st[:, mt, 0:1])
                    ACT(st[:, mt, 1:2], st[:, mt, 0:1], AF.Sqrt, bias=RMS_EPS, scale=1.0 / D, r=[tag + "ss%d" % mt], w=[tag + "sd%d" % mt])
                    P.emit("dve", lambda e, mt=mt: e.reciprocal(out=st[:, mt, 2:3], in_=st[:, mt, 1:2]),
                           reads=[tag + "sd%d" % mt], writes=[tag + "rs%d" % mt])
                    STT(xr[mt][:], xr[mt][:], st[:, mt, 2:3], grow, ALU.mult, ALU.mult, r=[xk, tag + "rs%d" % mt, "growf"], w=[xk])
                    DMA("sp", y_d[rows, :], xr[mt][:], "%s_st%d" % (tag, mt), r=[xk], w=["y_all"])
        ar.reset(m0, P)

    ar.reset(gmark0, P)
    if "B" in phases:
        P.emit("sp", lambda e: e.nop(), reads=["oT_d%d" % p_ for p_ in pairs], writes=["oT_all"])
        out_proj(oT_d, rwout_d, lambda mt: x_d[mt * 128:(mt + 1) * 128, :], False, "B")

    if "C" in phases:
        emit_conv(len(conv_jobs))
        P.barrier()
        NMT = nmt1
        TT1 = NMT * 128
        NTL = T // TT1
        grow1 = ar.alloc("grow1", [128, D], F32)
        DMA("sp", grow1[:], rowp_d[:, D:2 * D], "grow1", w=["grow"])
        wsT = ar.alloc("wsT", [128, 16, 128], BF16)
        DMA("pool", wsT[:], gwsT_d.rearrange("g j i -> j g i"), "wsT", w=["wsT"])
        biasf = ar.alloc("biasf", [128, NPAIR, 128], F32)
        mC = ar.mark()
        bsb = ar.alloc("bsb", [128, 16, 128], F32)
        DMA("sp", bsb[:], gbs_d.rearrange("p (g i) -> p g i", i=128), "bsb", w=["bsb"])
        ones_b = ar.alloc("ones_b", [128, 128], BF16)
        MEMSET("pool", ones_b[:], 1.0, w=["ones_b"])
        for g in range(16):
            MM(psb[2][:, 0:128], ones_b[:], wsT[:, g, :], r=["ones_b", "wsT"], w=["psb2"])
            for j in range(2):
                fc = 2 * g + j
                STT(biasf[:, fc, :], psb[2][:, 0:128], pcol(PB_GLB, fc), bsb[:, g, :], ALU.mult, ALU.add,
                    r=["psb2", "pp", "bsb"], w=["biasf"])
        ar.reset(mC, P)
        hnT1 = ar.alloc("hnT1", [128, 16, TT1], BF16)
        sq = ar.alloc("sq1", [128, D], BF16)
        hn = [ar.alloc("hn1", [128, D], BF16) for _ in range(2)]
        st = ar.alloc("st1", [128, 4], F32)
        vh = [ar.alloc("vh", [128, E], BF16) for _ in range(NMT)]
        lst = ar.alloc("lst", [128, NMT, 24], F32)
        ug = ar.alloc("ug", [128, NPAIR, TT1], BF16)
        tu = [ar.alloc("tu", [128, TT1], F32) for _ in range(2)]
        tg = [ar.alloc("tg", [128, TT1], F32) for _ in range(2)]
        wv = [ar.alloc("wv", [128, 16, 256], BF16) for _ in range(2)]
        wug = [ar.alloc("wug", [128, 2, 16, 128], BF16) for _ in range(2)]
        wo2 = [ar.alloc("wo2", [128, NPAIR, 256], BF16) for _ in range(2)]
        h2 = [ar.alloc("h2", [128, D], F32) for _ in range(NMT)]
        growf = ar.alloc("growf", [128, D], F32)
        DMA("sp", growf[:], rowp_d[:, 2 * D:3 * D], "growf", w=["growf"])
        gwin_v = gwin_b.rearrange("(kc kk) n -> kk kc n", kk=128)
        gwo_v = gwout_b.rearrange("(c p) n -> p c n", p=128)
        for tl in range(NTL):
            t0 = tl * TT1
            for mt in range(NMT):
                DMA("sp", h2[mt][:], h1_d[t0 + mt * 128:t0 + (mt + 1) * 128, :], "ld_h2_%d" % mt, r=["h1_all"], w=["h2_%d" % mt])
            norm_transpose(None, NMT, grow1[:], hnT1, 0, (None, sq, hn, st), "n1",
                           pre=[(h2[mt], "h2_%d" % mt) for mt in range(NMT)])
            for mt in range(NMT):
                MEMSET("pool", lst[:, mt, 8:24], 0.0, w=["lsp%d" % mt])
            for vb in range(16):
                s = vb % 2
                DMA("sp", wv[s][:], gwin_v[:, :, E + vb * 256:E + (vb + 1) * 256], "wv%d" % s, w=["wv%d" % s])
                for mt in range(NMT):
                    pb = psb[mt % 2]
                    for kc in range(16):
                        MM(pb[:, 0:256], hnT1[:, kc, mt * 128:(mt + 1) * 128], wv[s][:, kc, :], start=(kc == 0), stop=(kc == 15),
                           r=["hnT", "wv%d" % s], w=["psb%d" % (mt % 2)])
                    ACT(vh[mt][:, vb * 256:(vb + 1) * 256], pb[:, 0:256], AF.Gelu_apprx_tanh, r=["psb%d" % (mt % 2), "lsp%d" % mt],
                        w=["vh%d" % mt, "lsp%d" % mt], accum=lst[:, mt, 8 + vb:9 + vb])
            for mt in range(NMT):
                P.emit("dve", lambda e, mt=mt: e.tensor_reduce(out=lst[:, mt, 0:1], in_=lst[:, mt, 8:24], axis=AX.X, op=ALU.add),
                       reads=["lsp%d" % mt], writes=["ls0_%d" % mt])
                MEMSET("pool", lst[:, mt, 1:2], 0.0, w=["ls1_%d" % mt])
                ACT(sq[:], vh[mt][:, 0:D], AF.Square, r=["vh%d" % mt, "ls1_%d" % mt], w=["n1_sq", "ls1_%d" % mt], accum=lst[:, mt, 1:2])
                MEMSET("pool", lst[:, mt, 6:7], 0.0, w=["ls6_%d" % mt])
                ACT(sq[:], vh[mt][:, D:E], AF.Square, r=["vh%d" % mt, "ls6_%d" % mt], w=["n1_sq", "ls6_%d" % mt], accum=lst[:, mt, 6:7])
                TTo("dve", lst[:, mt, 1:2], lst[:, mt, 1:2], lst[:, mt, 6:7], ALU.add, r=["ls1_%d" % mt, "ls6_%d" % mt], w=["ls1_%d" % mt])
                TS("dve", lst[:, mt, 2:3], lst[:, mt, 0:1], 1.0 / E, None, ALU.mult, r=["ls0_%d" % mt], w=["ls2_%d" % mt])
                TTo("dve", lst[:, mt, 3:4], lst[:, mt, 2:3], lst[:, mt, 2:3], ALU.mult, r=["ls2_%d" % mt], w=["ls3_%d" % mt])
                STT(lst[:, mt, 4:5], lst[:, mt, 1:2], 1.0 / E, lst[:, mt, 3:4], ALU.mult, ALU.subtract,
                    r=["ls1_%d" % mt, "ls3_%d" % mt], w=["ls4_%d" % mt])
                ACT(lst[:, mt, 4:5], lst[:, mt, 4:5], AF.Sqrt, bias=LN_EPS, r=["ls4_%d" % mt], w=["ls4_%d" % mt])
                P.emit("dve", lambda e, mt=mt: e.reciprocal(out=lst[:, mt, 5:6], in_=lst[:, mt, 4:5]),
                       reads=["ls4_%d" % mt], writes=["ls5_%d" % mt])
                TS("dve", vh[mt][:], vh[mt][:], lst[:, mt, 2:3], lst[:, mt, 5:6], ALU.subtract, ALU.mult,
                   r=["ls2_%d" % mt, "ls5_%d" % mt, "vh%d" % mt], w=["vh%d" % mt])
            for fc in range(NPAIR):
                s = fc % 2
                g = fc // 2
                DMA("sp", wug[s][:, 0, :, :], gwin_v[:, :, fc * 128:(fc + 1) * 128], "wu%d" % s, w=["wu%d" % s])
                DMA("sp", wug[s][:, 1, :, :], gwin_v[:, :, 2 * E + fc * 128:2 * E + (fc + 1) * 128], "wg%d" % s, w=["wg%d" % s])
                for kc in range(16):
                    MM(psb[2][:, 0:TT1], wug[s][:, 0, kc, :], hnT1[:, kc, :], start=(kc == 0), stop=(kc == 15), r=["wu%d" % s, "hnT"], w=["psb2"])
                for kc in range(16):
                    MM(psb[3][:, 0:TT1], wug[s][:, 1, kc, :], hnT1[:, kc, :], start=(kc == 0), stop=(kc == 15), r=["wg%d" % s, "hnT"], w=["psb3"])
                ACT(tu[s][:], psb[2][:, 0:TT1], AF.Gelu_apprx_tanh, r=["psb2"], w=["tu%d" % s])
                ACT(tg[s][:], psb[3][:, 0:TT1], AF.Silu, r=["psb3"], w=["tg%d" % s])
                TTo("pool", ug[:, fc, :], tu[s][:], tg[s][:], ALU.mult, r=["tu%d" % s, "tg%d" % s], w=["ug%d" % fc])
                for mt in range(NMT):
                    MM(psb[4 + s][:, mt * 128:(mt + 1) * 128], vh[mt][:, fc * 128:(fc + 1) * 128], wsT[:, g, :],
                       r=["vh%d" % mt, "wsT"], w=["psb%d" % (4 + s)])
                STT(tu[s][:].rearrange("p (m i) -> p m i", i=128), psb[4 + s][:, 0:TT1].rearrange("p (m i) -> p m i", i=128),
                    pcol(PB_GLG, fc), biasf[:, fc, None, :].broadcast_to([128, NMT, 128]), ALU.mult, ALU.add,
                    r=["psb%d" % (4 + s), "pp", "biasf", "ug%d" % fc], w=["tu%d" % s])
                TTo("pool", ug[:, fc, :], tu[s][:], ug[:, fc, :], ALU.mult, r=["tu%d" % s, "ug%d" % fc], w=["ug%d" % fc])
            for nb in range(8):
                s = nb % 2
                DMA("sp", wo2[s][:], gwo_v[:, :, nb * 256:(nb + 1) * 256], "wo2_%d" % s, w=["wo2_%d" % s])
                for mt in range(NMT):
                    po = psb[6 + mt % 2][:, 0:256]
                    pk = "psb%d" % (6 + mt % 2)
                    for c in range(NPAIR):
                        MM(po, ug[:, c, mt * 128:(mt + 1) * 128], wo2[s][:, c, :], start=(c == 0), stop=(c == NPAIR - 1),
                           r=["ug%d" % c, "wo2_%d" % s], w=[pk])
                    TTo("dve", h2[mt][:, nb * 256:(nb + 1) * 256], po, h2[mt][:, nb * 256:(nb + 1) * 256], ALU.add,
                        r=[pk, "h2_%d" % mt], w=["h2_%d" % mt])
            for mt in range(NMT):
                rows = slice(t0 + mt * 128, t0 + (mt + 1) * 128)
                hk = "h2_%d" % mt
                MEMSET("pool", lst[:, mt, 6:7], 0.0, w=["fs0_%d" % mt])
                ACT(sq[:], h2[mt][:], AF.Square, r=[hk, "fs0_%d" % mt], w=["n1_sq", "fs0_%d" % mt], accum=lst[:, mt, 6:7])
                ACT(lst[:, mt, 7:8], lst[:, mt, 6:7], AF.Sqrt, bias=RMS_EPS, scale=1.0 / D, r=["fs0_%d" % mt], w=["fs1_%d" % mt])
                P.emit("dve", lambda e, mt=mt: e.reciprocal(out=lst[:, mt, 6:7], in_=lst[:, mt, 7:8]),
                       reads=["fs1_%d" % mt], writes=["fs0_%d" % mt])
                STT(h2[mt][:], h2[mt][:], lst[:, mt, 6:7], growf[:], ALU.mult, ALU.mult, r=[hk, "fs0_%d" % mt, "growf"], w=[hk])
                DMA("sp", y_d[rows, :], h2[mt][:], "st_y%d" % mt, r=[hk], w=["y_all"])
        ar.reset(gmark0, P)

    P.emit("sp", lambda e: e.nop(), reads=["y_all", "h1_all"] + ["oT_d%d" % p_ for p_ in pairs])

    P.finalize()
    from contextlib import ExitStack
    with ExitStack() as es:
        sems = {e: es.enter_context(nc.semaphore("s_" + e)) for e in ENGS}
        dsems = {n: es.enter_context(nc.semaphore("d_" + n)) for n in sorted(P.dma_count)}
        with nc.Block() as block:
            @block.sync
            def _(e):
                P.replay("sp", e, sems, dsems)

            @block.scalar
            def _(e):
                P.replay("act", e, sems, dsems)

            @block.vector
            def _(e):
                P.replay("dve", e, sems, dsems)

            @block.gpsimd
            def _(e):
                P.replay("pool", e, sems, dsems)

            @block.tensor
            def _(e):
                P.replay("pe", e, sems, dsems)
    nc._dbg_out = dbg_out
    nc._prog = P
    nc._arena_peak = ar.peak
    return nc


def make_in_maps(inp):
    pp, rowp, gbs = _host_params(inp)
    cbm, cfm = _consts()
    shared = {
        "pp": pp, "rowp": rowp, "cb": cbm, "cf": cfm,
        "rw_in": np.ascontiguousarray(inp["rwkv_w_in"][0], np.float32),
        "rw_up": np.ascontiguousarray(inp["rwkv_w_up"][0], np.float32),
        "ra_up": np.ascontiguousarray(inp["rwkv_a_up"][0], np.float32),
        "rw_out": np.ascontiguousarray(inp["rwkv_w_out"][0], np.float32),
        "gw_in": np.ascontiguousarray(inp["gmlp_w_in"][0], np.float32),
        "gw_sT": np.ascontiguousarray(np.transpose(inp["gmlp_w_s"][0], (0, 2, 1)), np.float32),
        "gbs": gbs,
        "gw_out": np.ascontiguousarray(inp["gmlp_w_out"][0], np.float32),
    }
    x = np.asarray(inp["x"], np.float32)
    return [dict(shared, x=np.ascontiguousarray(x[b])) for b in range(x.shape[0])]


def kernel(**inputs):
    inp = {k: np.asarray(v) for k, v in inputs.items()}
    nc = build()
    in_maps = make_in_maps(inp)
    res = run_bass_kernel_spmd(nc, in_maps, core_ids=list(range(8)))
    return np.stack([np.asarray(r["y"], np.float32) for r in res.results], axis=0)
```
